# Optimizing a Trainium2 kernel written in Bass

```python
import jax, jax.numpy as jnp
from jax import lax
import numpy as np

D_MODEL = 2048
BATCH = 16
SEQ = 256
DEPTH = 2
DEC_BATCH = 2
DEC_SEQ = 1024
PAST_LEN = 512

GRID_W = 64
HEAD_DIM = 128
W_A = D_MODEL // 4
W_B = D_MODEL // 4
W_C = D_MODEL // 4
W_D = D_MODEL // 4
N_HEADS_A = W_A // HEAD_DIM
N_HEADS_B = W_B // HEAD_DIM
N_GROUPS_D = 4
GROUP_D = W_D // N_GROUPS_D
POOL_WINDOWS = (2, 4, 8, 16)
CHUNK_A = 64
CHUNK_B = 128
CONV_W = 3
D_FF = 5632
EPS = 1e-6
SPLIT_SIZES = (W_A,) * 5 + (W_B,) * 2 + (W_C,) * 3 + (W_D,)
D_IN = sum(SPLIT_SIZES)
SPLIT_POINTS = [int(v) for v in np.cumsum(SPLIT_SIZES)[:-1]]

kernel_name = 'hymba_style_diffusion_hgrn2_sgu_conv_pool_step'


def rmsnorm(x, g):
    x32 = x.astype(jnp.float32)
    y = x32 * lax.rsqrt(jnp.mean(x32 * x32, axis=-1, keepdims=True) + EPS)
    return (y * g.astype(jnp.float32)).astype(x.dtype)


def adaln(cvec, w_ada, b_ada):
    m = (jax.nn.silu(cvec) @ w_ada + b_ada)[:, None, :]
    return jnp.split(m, 6, axis=-1)


def dwconv3(x, w):
    return lax.conv_general_dilated(
        x, w[:, None, :].astype(x.dtype), window_strides=(1,),
        padding=[(CONV_W // 2, CONV_W // 2)],
        dimension_numbers=('NWC', 'WIO', 'NWC'), feature_group_count=x.shape[-1])


def grid_pos_embed(n_tokens, dim, dtype):
    rows = n_tokens // GRID_W
    r = jnp.repeat(jnp.arange(rows), GRID_W).astype(jnp.float32)[:, None]
    col = jnp.tile(jnp.arange(GRID_W), rows).astype(jnp.float32)[:, None]
    quarter = dim // 4
    freq = jnp.exp(-jnp.log(10000.0) * jnp.arange(quarter, dtype=jnp.float32) / quarter)[None, :]
    emb = jnp.concatenate([jnp.sin(r * freq), jnp.cos(r * freq), jnp.sin(col * freq), jnp.cos(col * freq)], -1)
    return emb.astype(dtype)


def lower_bounds(lb_logits):
    p = jax.nn.softmax(lb_logits.astype(jnp.float32), axis=0)
    return jnp.cumsum(p, axis=0) - p[0:1]


def hgrn_chunk_scan(q, k, v, logf, s0):
    b, h, t, dk = q.shape
    n = t // CHUNK_A
    def to_chunks(a):
        return a.reshape(b, h, n, CHUNK_A, a.shape[-1]).transpose(2, 0, 1, 3, 4)
    mask = jnp.tril(jnp.ones((CHUNK_A, CHUNK_A), dtype=bool))[:, :, None]
    def step(s, inp):
        qc, kc, vc, lc = inp
        cum = jnp.cumsum(lc, axis=2)
        o_inter = jnp.einsum('bhtk,bhkv->bhtv', qc * jnp.exp(cum), s)
        diff = cum[:, :, :, None, :] - cum[:, :, None, :, :]
        decay = jnp.exp(jnp.where(mask, diff, -jnp.inf))
        scores = jnp.einsum('bhtk,bhsk,bhtsk->bhts', qc, kc, decay)
        o = o_inter + jnp.einsum('bhts,bhsv->bhtv', scores, vc)
        last = cum[:, :, -1:, :]
        s_new = jnp.exp(last[:, :, 0, :])[..., None] * s + jnp.einsum('bhsk,bhsv->bhkv', kc * jnp.exp(last - cum), vc)
        return s_new, o
    s_fin, o = lax.scan(step, s0.astype(jnp.float32), (to_chunks(q), to_chunks(k), to_chunks(v), to_chunks(logf)))
    o = o.transpose(1, 2, 0, 3, 4).reshape(b, h, t, v.shape[-1])
    return o, s_fin


def hgrn2_bidir(q, i, zf, zb, g, lb_f, lb_b, s0_f, s0_b, norm_g):
    bsz, t, _ = q.shape
    def heads(a):
        return a.astype(jnp.float32).reshape(bsz, t, N_HEADS_A, HEAD_DIM).transpose(0, 2, 1, 3)
    qh, ih = heads(q), heads(i)
    def gates(z, lb):
        lb = lb.reshape(1, N_HEADS_A, 1, HEAD_DIM)
        f = lb + (1.0 - lb) * jax.nn.sigmoid(heads(z))
        return 1.0 - f, jnp.log(f)
    kf, lff = gates(zf, lb_f)
    kb, lfb = gates(zb, lb_b)
    o_f, s_f = hgrn_chunk_scan(qh, kf, ih, lff, s0_f)
    rev = lambda a: jnp.flip(a, axis=2)
    o_b, s_b = hgrn_chunk_scan(rev(qh), rev(kb), rev(ih), rev(lfb), s0_b)
    o = (o_f + rev(o_b)).transpose(0, 2, 1, 3)
    o = o * lax.rsqrt(jnp.mean(o * o, axis=-1, keepdims=True) + EPS)
    o = o.reshape(bsz, t, W_A) * norm_g.astype(jnp.float32) * jax.nn.silu(g.astype(jnp.float32))
    return o.astype(q.dtype), s_f.astype(q.dtype), s_b.astype(q.dtype)


def chunk_sgu(u, v, sgu_norm, w_sgu, b_sgu):
    bsz, t, _ = u.shape
    u = jax.nn.gelu(u)
    v = rmsnorm(jax.nn.gelu(v), sgu_norm)
    vh = v.reshape(bsz, t // CHUNK_B, CHUNK_B, N_HEADS_B, HEAD_DIM)
    mixed = jnp.einsum('hps,bnshc->bnphc', w_sgu, vh) + b_sgu.T[None, None, :, :, None]
    return u * mixed.reshape(bsz, t, W_B)


def centred_window_mean(x, w):
    bsz, t, ch = x.shape
    csum = jnp.concatenate([jnp.zeros((bsz, 1, ch), jnp.float32), jnp.cumsum(x.astype(jnp.float32), axis=1)], axis=1)
    pos = jnp.arange(t)
    lo = jnp.clip(pos - w // 2, 0, t)
    hi = jnp.clip(pos + w // 2, 0, t)
    s = jnp.take(csum, hi, axis=1) - jnp.take(csum, lo, axis=1)
    cnt = (hi - lo).astype(jnp.float32)[None, :, None]
    return (s / cnt).astype(x.dtype)


def multiscale_pool(x, w_pool, pool_scale):
    outs = []
    for gi, win in enumerate(POOL_WINDOWS):
        xg = x[..., gi * GROUP_D:(gi + 1) * GROUP_D]
        outs.append((centred_window_mean(xg, win) - xg) @ w_pool[gi])
    return jnp.concatenate(outs, axis=-1) * pool_scale


def token_mixers(h, s0_f, s0_b, p):
    z = h @ p['w_in']
    a_q, a_i, a_ff, a_fb, a_g, b_u, b_v, c_b, c_c, c_h, d_x = jnp.split(z, SPLIT_POINTS, axis=-1)
    a_out, s_f, s_b = hgrn2_bidir(a_q, a_i, a_ff, a_fb, a_g, p['lb_f'], p['lb_b'], s0_f, s0_b, p['hgrn_norm'])
    b_out = chunk_sgu(b_u, b_v, p['sgu_norm'], p['w_sgu'], p['b_sgu'])
    c_out = c_b * dwconv3(c_c * c_h, p['w_conv_c'])
    d_out = multiscale_pool(d_x, p['w_pool'], p['pool_scale'])
    y = jnp.concatenate([a_out, b_out, c_out, d_out], axis=-1) @ p['w_out']
    return y, s_f, s_b


def conv_ffn(h, w_up, w_conv, w_down):
    a = dwconv3(h @ w_up, w_conv)
    val, gate = jnp.split(a, 2, axis=-1)
    return (val * jax.nn.silu(gate)) @ w_down


def trunk_layer(x, mod, s0_f, s0_b, p):
    sh1, sc1, g1, sh2, sc2, g2 = mod
    h = rmsnorm(x, p['norm_mix']) * (1.0 + sc1) + sh1
    m, s_f, s_b = token_mixers(h, s0_f, s0_b, p)
    x = x + g1 * m
    h = rmsnorm(x, p['norm_ffn']) * (1.0 + sc2) + sh2
    x = x + g2 * conv_ffn(h, p['w_up'], p['w_conv_ffn'], p['w_down'])
    return x, s_f, s_b


def setup_inputs(seed: int = 0) -> dict:
    key = jax.random.key(seed)
    ks = jax.random.split(key, 24)
    f32 = jnp.float32
    def nrm(k, shape, scale):
        return jax.random.normal(k, shape, f32) * scale
    return {
        'x_prompt': nrm(ks[0], (BATCH, SEQ, D_MODEL), 1.0),
        'x_sample': nrm(ks[1], (DEC_BATCH, DEC_SEQ, D_MODEL), 1.0),
        'c': nrm(ks[2], (DEC_BATCH, D_MODEL), 1.0),
        'state_hgrn': nrm(ks[3], (DEC_BATCH, DEPTH, 2, N_HEADS_A, HEAD_DIM, HEAD_DIM), 0.5),
        'c_ctx': nrm(ks[4], (D_MODEL,), 1.0),
        'w_ada': nrm(ks[5], (DEPTH, D_MODEL, 6 * D_MODEL), D_MODEL ** -0.5),
        'b_ada': nrm(ks[6], (DEPTH, 6 * D_MODEL), 0.02),
        'norm_mix': 1.0 + nrm(ks[7], (DEPTH, D_MODEL), 0.02),
        'norm_ffn': 1.0 + nrm(ks[8], (DEPTH, D_MODEL), 0.02),
        'w_in': nrm(ks[9], (DEPTH, D_MODEL, D_IN), D_MODEL ** -0.5),
        'lb_logits': nrm(ks[10], (DEPTH, 2, W_A), 0.5),
        'hgrn_norm': 1.0 + nrm(ks[11], (DEPTH, W_A), 0.02),
        'sgu_norm': 1.0 + nrm(ks[12], (DEPTH, W_B), 0.02),
        'w_sgu': nrm(ks[13], (DEPTH, N_HEADS_B, CHUNK_B, CHUNK_B), CHUNK_B ** -0.5),
        'b_sgu': nrm(ks[14], (DEPTH, N_HEADS_B, CHUNK_B), 0.02),
        'w_conv_c': nrm(ks[15], (DEPTH, CONV_W, W_C), CONV_W ** -0.5),
        'w_pool': nrm(ks[16], (DEPTH, N_GROUPS_D, GROUP_D, GROUP_D), GROUP_D ** -0.5),
        'pool_scale': 1.0 + nrm(ks[17], (DEPTH, W_D), 0.02),
        'w_out': nrm(ks[18], (DEPTH, D_MODEL, D_MODEL), D_MODEL ** -0.5),
        'w_up': nrm(ks[19], (DEPTH, D_MODEL, 2 * D_FF), D_MODEL ** -0.5),
        'w_conv_ffn': nrm(ks[20], (DEPTH, CONV_W, 2 * D_FF), CONV_W ** -0.5),
        'w_down': nrm(ks[21], (DEPTH, D_FF, D_MODEL), D_FF ** -0.5),
        'norm_final': 1.0 + nrm(ks[22], (D_MODEL,), 0.02),
    }


def reference(x_prompt, x_sample, c, state_hgrn, c_ctx, w_ada, b_ada, norm_mix, norm_ffn, w_in,
              lb_logits, hgrn_norm, sgu_norm, w_sgu, b_sgu, w_conv_c, w_pool, pool_scale, w_out,
              w_up, w_conv_ffn, w_down, norm_final):
    lbs = lower_bounds(lb_logits)
    xp = x_prompt
    xs = x_sample + grid_pos_embed(x_sample.shape[1], x_sample.shape[2], x_sample.dtype)[None]
    zero_state = jnp.zeros((x_prompt.shape[0], N_HEADS_A, HEAD_DIM, HEAD_DIM), jnp.float32)
    new_states = []
    for l in range(DEPTH):
        p = {
            'norm_mix': norm_mix[l], 'norm_ffn': norm_ffn[l], 'w_in': w_in[l],
            'lb_f': lbs[l, 0], 'lb_b': lbs[l, 1], 'hgrn_norm': hgrn_norm[l],
            'sgu_norm': sgu_norm[l], 'w_sgu': w_sgu[l], 'b_sgu': b_sgu[l],
            'w_conv_c': w_conv_c[l], 'w_pool': w_pool[l], 'pool_scale': pool_scale[l],
            'w_out': w_out[l], 'w_up': w_up[l], 'w_conv_ffn': w_conv_ffn[l], 'w_down': w_down[l],
        }
        mod_ctx = adaln(c_ctx[None, :], w_ada[l], b_ada[l])
        mod_lat = adaln(c, w_ada[l], b_ada[l])
        xp, s_f, s_b = trunk_layer(xp, mod_ctx, zero_state, zero_state, p)
        xs, _, _ = trunk_layer(xs, mod_lat, state_hgrn[:, l, 0], state_hgrn[:, l, 1], p)
        new_states.append(jnp.stack([s_f, s_b], axis=1))
    new_state_hgrn = jnp.stack(new_states, axis=1)
    y_prompt = rmsnorm(xp, norm_final)
    y_sample = rmsnorm(xs, norm_final)
    return (y_prompt, y_sample, new_state_hgrn)
```

```python
from contextlib import ExitStack, nullcontext
import numpy as np
import concourse.bass as bass
import concourse.mybir as mybir
from concourse.bass_utils import run_bass_kernel_spmd

F32 = mybir.dt.float32
BF16 = mybir.dt.bfloat16
AF = mybir.ActivationFunctionType
OP = mybir.AluOpType

D = 2048
DEPTH = 2
DFF = 5632
NCORES = 8
TP = 256
TS = 1024
TTOT = 2 * TP + TS
EPS = 1e-6
POOLW = (2, 4, 8, 16)
V_BADA, V_NM, V_NF, V_HN, V_WC, V_PS, V_WCF, V_LBL = 0, 96, 112, 128, 132, 144, 148, 412
V_PER = 420
V_FIN = 2 * V_PER
NV = V_FIN + 16


class Sch:
    def __init__(self, nc, st):
        self.nc = nc
        self.st = st
        self.E = {}
        for n, eng in [("pe", nc.tensor), ("dve", nc.vector), ("act", nc.scalar),
                       ("pool", nc.gpsimd), ("sp", nc.sync)]:
            sem = st.enter_context(nc.semaphore("s_" + n))
            self.E[n] = dict(eng=eng, sem=sem, cnt=0, waited={})
        self.dsem = {}
        self.lw = {}
        self.rd = {}

    def semof(self, src):
        if src in self.E:
            return self.E[src]["sem"]
        return self.dsem[src][0]

    def _wait(self, en, need):
        e = self.E[en]
        for src, val in need.items():
            if src == en and en == "pe":
                continue
            if e["waited"].get(src, 0) < val:
                e["eng"].wait_ge(self.semof(src), val)
                e["waited"][src] = val

    def _deps(self, r, w):
        need = {}

        def add(t):
            if t is not None:
                need[t[0]] = max(need.get(t[0], 0), t[1])
        for k in r:
            add(self.lw.get(k))
        for k in w:
            add(self.lw.get(k))
            for s_, v_ in self.rd.get(k, {}).items():
                add((s_, v_))
        return need

    def _reg(self, src, tval, r, w):
        for k in r:
            d = self.rd.setdefault(k, {})
            d[src] = max(d.get(src, 0), tval)
        for k in w:
            self.lw[k] = (src, tval)
            self.rd[k] = {}

    def op(self, en, fn, r=(), w=(), inc=True):
        e = self.E[en]
        self._wait(en, self._deps(r, w))
        ins = fn(e["eng"])
        if inc:
            e["cnt"] += 1
            ins.then_inc(e["sem"], 1)
            tval = e["cnt"]
        else:
            tval = e["cnt"] + 1
        self._reg(en, tval, r, w)

    def dma(self, q, out, in_, r=(), w=(), sem="d0", accum=False):
        if sem not in self.dsem:
            self.dsem[sem] = [self.st.enter_context(self.nc.semaphore("dq_" + sem)), 0]
        ds = self.dsem[sem]
        need = self._deps(r, w)
        if ds[1] > 0:
            need[sem] = max(need.get(sem, 0), ds[1])
        self._wait(q, need)
        if accum:
            self.E[q]["eng"].dma_start(out=out, in_=in_, accum_op=OP.add).then_inc(ds[0], 16)
        else:
            self.E[q]["eng"].dma_start(out=out, in_=in_).then_inc(ds[0], 16)
        ds[1] += 16
        self._reg(sem, ds[1], r, w)

    def barrier(self):
        for en in self.E:
            need = {o: self.E[o]["cnt"] for o in self.E if self.E[o]["cnt"] > 0}
            for s_, d_ in self.dsem.items():
                if d_[1] > 0:
                    need[s_] = d_[1]
            self._wait(en, need)
        self.lw = {}
        self.rd = {}


def tblocks(T):
    return [(t0, min(512, T - t0)) for t0 in range(0, T, 512)]


def build_program(flags=()):
    nc = bass.Bass("TRN2", target_bir_lowering=False)

    def din(name, shape):
        return nc.dram_tensor(name, list(shape), F32, kind="ExternalInput").ap()
    xT = din("xT", [D, TTOT])
    posT = din("posT", [D, TS])
    cv = din("cv", [128, 16, 2])
    st0 = din("st0", [DEPTH, 2, 4, 128, 128])
    vecs = din("vecs", [128, NV])
    sgn = din("sgn", [DEPTH, 512])
    bsg = din("bsg", [DEPTH, 512])
    wsgT = din("wsgT", [DEPTH, 4, 128, 128])
    wpool = din("wpool", [DEPTH, 4, 128, 128])
    consts = din("consts", [128, 384])
    w_ada = din("w_ada", [DEPTH, 96, 128, 16 * 128])
    w_in = din("w_in", [DEPTH, 44, 128, 16 * 128])
    w_out = din("w_out", [DEPTH, 16, 128, 16 * 128])
    w_up = din("w_up", [DEPTH, 88, 128, 16 * 128])
    w_down = din("w_down", [DEPTH, 16, 128, 44 * 128])
    yT = nc.dram_tensor("yT", [D, TTOT], F32, kind="ExternalOutput").ap()
    nst = nc.dram_tensor("nst", [2, DEPTH, 2, 4, 128, 128], F32, kind="ExternalOutput").ap()

    uid = [0]

    with ExitStack() as st:
        S = Sch(nc, st)

        def sb(stack, name, shape, dt=F32):
            uid[0] += 1
            return stack.enter_context(nc.sbuf_tensor(f"{name}_{uid[0]}", list(shape), dt))

        def ps(name, shape, dt=F32):
            return st.enter_context(nc.psum_tensor(name, list(shape), dt))

        X = sb(st, "X", [128, 16, TS])
        H = sb(st, "H", [128, 16, TS], BF16)
        NW = 2
        WB = [sb(st, f"WB{i}", [128, 4096], BF16) for i in range(NW)]
        VEC = sb(st, "VEC", [128, NV])
        MOD = [sb(st, f"MOD{l}", [128, 96, 2]) for l in range(DEPTH)]
        A1 = [sb(st, f"A1{l}", [128, 16, 2]) for l in range(DEPTH)]
        A2 = [sb(st, f"A2{l}", [128, 16, 2]) for l in range(DEPTH)]
        LB = sb(st, "LB", [128, 16])
        OML = sb(st, "OML", [128, 16])
        CST = sb(st, "CST", [128, 384])
        IDB = sb(st, "IDB", [128, 128], BF16)
        ONB = sb(st, "ONB", [128, 128], BF16)
        EPSC = sb(st, "EPSC", [128, 1])
        ONEC = sb(st, "ONEC", [128, 1])
        ONEF = sb(st, "ONEF", [128, TS], BF16)
        SC = sb(st, "SC", [128, 16, 2], BF16)
        CVT = sb(st, "CVT", [128, 16, 2])
        ADR = sb(st, "ADR", [2, 256])
        SQ = [sb(st, f"SQ{i}", [128, 512], BF16) for i in range(2)]
        RS = sb(st, "RS", [128, 512])
        RT = sb(st, "RT", [128, 512])
        TM = [sb(st, f"TM{i}", [128, 512]) for i in range(2)]
        MASKF = CST[0:32, 128:160]
        MASKB = CST[0:32, 256:288]

        PA = [ps(f"PA{i}", [128, 512]) for i in range(4)]
        PS_ = ps("PSs", [128, 512])
        PT = ps("PT", [128, 512])
        PTB = ps("PTB", [128, 512])
        PO = ps("PO", [128, 512])
        pa_i = [0]
        sq_i = [0]
        tm_i = [0]
        ws_i = [0]

        def V(fn, r, w):
            S.op("dve", fn, r, w)

        def A(fn, r, w):
            S.op("act", fn, r, w)

        def act(out, in_, func, r, w, bias=None, scale=1.0):
            if bias is None:
                S.op("act", lambda e: e.activation(out=out, in_=in_, func=func, scale=scale), r, w)
            else:
                S.op("act", lambda e: e.activation(out=out, in_=in_, func=func, bias=bias, scale=scale), r, w)

        def tt(out, in0, in1, op, r, w):
            V(lambda e: e.tensor_tensor(out=out, in0=in0, in1=in1, op=op), r, w)

        def ts(out, in0, s1, s2, op0, op1, r, w):
            V(lambda e: e.tensor_scalar(out=out, in0=in0, scalar1=s1, scalar2=s2, op0=op0, op1=op1), r, w)

        def stt(out, in0, scalar, in1, op0, op1, r, w):
            V(lambda e: e.scalar_tensor_tensor(out=out, in0=in0, scalar=scalar, in1=in1, op0=op0, op1=op1), r, w)

        def mm(out, lhsT, rhs, start, stop, r, w, inc):
            S.op("pe", lambda e: e.matmul(out, lhsT=lhsT, rhs=rhs, start=start, stop=stop), r, w, inc=inc)

        def load_w(srct, k0, nK, c0, ncols):
            slot = ws_i[0] % len(WB)
            ws_i[0] += 1
            nch = ncols // 128
            cc0 = c0 // 128
            view = WB[slot][:, 0:nch * nK * 128].rearrange("p (c k f) -> p c k f", c=nch, k=nK)
            srcv = srct[cc0:cc0 + nch, :, k0 * 128:(k0 + nK) * 128].rearrange("c p (k f) -> p c k f", k=nK)
            S.dma("pool", view, srcv, r=[], w=[f"WB{slot}"], sem=f"dw{slot}")
            return view, f"WB{slot}"

        def next_pa():
            i = pa_i[0] % 4
            pa_i[0] += 1
            return PA[i], f"PA{i}"

        def next_tm():
            i = tm_i[0] % 2
            tm_i[0] += 1
            return TM[i], f"TM{i}"

        def proj_fm(srct, k0, nK, c0, ncols, rhs_fn, T, evac):
            wv, wkey = load_w(srct, k0, nK, c0, ncols)
            for f in range(ncols // 128):
                for (t0, tn) in tblocks(T):
                    pa, pkey = next_pa()
                    for k in range(nK):
                        rhs, rkey = rhs_fn(k, t0, tn)
                        mm(pa[:, 0:tn], wv[:, f, k, :], rhs, k == 0, k == nK - 1,
                           [wkey, rkey], [pkey], k == nK - 1)
                    evac(f, t0, tn, pa[:, 0:tn], pkey)
            bg_step()

        def rstd_from(ss_ap, sskey, tn, n):
            act(RT[:, 0:tn], ss_ap, AF.Ln, [sskey], ["RT"], bias=EPSC[:, 0:1], scale=1.0 / n)
            act(RS[:, 0:tn], RT[:, 0:tn], AF.Exp, ["RT"], ["RS"], scale=-0.5)

        def sumsq_bcast(src_fn, nchunks, t0, tn):
            for c in range(nchunks):
                src, skey = src_fn(c)
                i = sq_i[0] % 2
                sq_i[0] += 1
                act(SQ[i][:, 0:tn], src, AF.Square, [skey], [f"SQ{i}"])
                mm(PS_[:, 0:tn], ONB[:, :], SQ[i][:, 0:tn], c == 0, c == nchunks - 1,
                   [f"SQ{i}", "ONB"], ["PS"], True)

        def norm_to_H(Aco, Bco_fn, T, vi, mkeys):
            for (t0, tn) in tblocks(T):
                sumsq_bcast(lambda c: (X[:, c, t0:t0 + tn], f"X{c}"), 16, t0, tn)
                rstd_from(PS_[:, 0:tn], "PS", tn, D)
                for c in range(16):
                    tm, tkey = next_tm()
                    tt(tm[:, 0:tn], X[:, c, t0:t0 + tn], RS[:, 0:tn], OP.mult, [f"X{c}", "RS"], [tkey])
                    act(H[:, c, t0:t0 + tn], tm[:, 0:tn], AF.Identity, [tkey] + mkeys, ["H"],
                        bias=Bco_fn(c), scale=Aco[:, c, vi:vi + 1])

        S.dma("sp", VEC[:, :], vecs[:, :], w=["VEC"], sem="l0")
        S.dma("sp", CST[:, :], consts[:, :], w=["CST"], sem="l1")
        S.dma("sp", CVT[:, :, :], cv[:, :, :], w=["CVT"], sem="l2")
        V(lambda e: e.tensor_copy(out=IDB[:, :], in_=CST[:, 0:128]), ["CST"], ["IDB"])
        V(lambda e: e.memset(ONB[:, :], 1.0), [], ["ONB"])
        V(lambda e: e.memset(EPSC[:, :], EPS), [], ["EPSC"])
        V(lambda e: e.memset(ONEC[:, :], 1.0), [], ["ONEC"])
        V(lambda e: e.memset(ONEF[:, :], 1.0), [], ["ONEF"])
        act(SC[:, :, :], CVT[:, :, :], AF.Silu, ["CVT"], ["SC"])
        V(lambda e: e.memset(LB[:, 0:8], 0.0), [], ["LB"])
        tt(LB[:, 8:16], VEC[:, V_PER + V_LBL:V_PER + V_LBL + 8], VEC[:, V_LBL:V_LBL + 8], OP.subtract, ["VEC"], ["LB"])
        act(LB[:, 8:16], LB[:, 8:16], AF.Sigmoid, ["LB"], ["LB"])
        ts(OML[:, :], LB[:, :], -1.0, 1.0, OP.mult, OP.add, ["LB"], ["OML"])

        mod_done = [0, 0]

        def ada_block(l, fb):
            vb = l * V_PER
            wv, wkey = load_w(w_ada[l], 0, 16, fb * 256, 256)
            pa, pkey = next_pa()
            for k in range(16):
                mm(pa[0:2, 0:256], SC[:, k, :], wv[:, :, k, :], k == 0, k == 15, [wkey, "SC"], [pkey], k == 15)
            act(ADR[:, :], pa[0:2, 0:256], AF.Copy, [pkey], ["ADR"])
            pb, pbkey = next_pa()
            for f in range(2):
                mm(pb[:, 2 * f:2 * f + 2], ADR[0:2, f * 128:(f + 1) * 128], CST[0:2, 0:2], True, True,
                   ["ADR", "CST"], [pbkey], f == 1)
            for f in range(2):
                ch = fb * 2 + f
                ts(MOD[l][:, ch, :], pb[:, 2 * f:2 * f + 2], VEC[:, vb + V_BADA + ch:vb + V_BADA + ch + 1], None,
                   OP.add, OP.bypass, [pbkey, "VEC"], [f"MOD{l}_{ch // 16}"])
            mod_done[l] = fb * 2 + 2
            if mod_done[l] == 32:
                for vi in range(2):
                    ts(A1[l][:, :, vi], MOD[l][:, 16:32, vi], 1.0, None, OP.add, OP.bypass, [f"MOD{l}_1"], [f"A1_{l}"])
                    tt(A1[l][:, :, vi], A1[l][:, :, vi], VEC[:, vb + V_NM:vb + V_NM + 16], OP.mult, [f"A1_{l}", "VEC"], [f"A1_{l}"])
            if mod_done[l] == 80:
                for vi in range(2):
                    ts(A2[l][:, :, vi], MOD[l][:, 64:80, vi], 1.0, None, OP.add, OP.bypass, [f"MOD{l}_4"], [f"A2_{l}"])
                    tt(A2[l][:, :, vi], A2[l][:, :, vi], VEC[:, vb + V_NF:vb + V_NF + 16], OP.mult, [f"A2_{l}", "VEC"], [f"A2_{l}"])

        def ada_gen():
            for l_ in range(DEPTH):
                for fb_ in range(48):
                    ada_block(l_, fb_)
                    yield

        bg = ada_gen()

        def bg_step():
            next(bg, None)

        def bg_need(l, nchunks):
            while mod_done[l] < nchunks:
                if next(bg, "END") == "END":
                    break

        out_sems = []

        seqs = [(2 * TP, TS, 1, [(0, TS, -1)]), (0, 2 * TP, 0, [(0, TP, 0), (TP, TP, 1)])]
        if 'noprompt' in flags:
            seqs = seqs[:1]
        if 'nosample' in flags:
            seqs = seqs[1:]
        if 'adaonly' in flags:
            seqs = []
        for (tok0, T, vi, segs) in seqs:
            TB = tblocks(T)
            gscope = st.enter_context(ExitStack())
            CH, HF = 32, 16
            NPB = 512 // CH
            NCH = T // CH
            xTv = xT.rearrange("(c p) t -> p c t", p=128)
            for c in range(16):
                S.dma("sp", X[:, c, 0:T], xTv[:, c, tok0:tok0 + T], w=[f"X{c}"], sem=f"lx{c % 4}")
            if vi == 1:
                pv = posT.rearrange("(c p) t -> p c t", p=128)
                for c in range(16):
                    S.dma("pool", X[:, c, 0:T], pv[:, c, 0:T], w=[f"X{c}"], sem=f"lp{c % 2}", accum=True)

            def rhsH(k, t0, tn):
                return H[:, k, t0:t0 + tn], "H"

            for l in range(DEPTH):
                vb = l * V_PER
                W_in = w_in[l]
                with ExitStack() as ms:
                    YC = sb(ms, "YC", [128, 16, T], BF16)
                    bg_need(l, 32)
                    norm_to_H(A1[l], lambda c: MOD[l][:, c, vi:vi + 1], T, vi, [f"A1_{l}", f"MOD{l}_0"])

                    big = T > 512
                    with (ExitStack() if big else nullcontext(ms)) as hs:
                        SG = sb(hs, "SG", [128, T], BF16)
                        V32 = sb(hs, "V32", [CH, NCH, 128], BF16)
                        VF = sb(hs, "VF", [128, T], BF16)
                        Q = sb(hs, "Q", [128, T], BF16)
                        O = sb(hs, "O", [128, T])
                        CE = sb(hs, "CE", [128, T + 1])
                        KK = sb(hs, "KK", [128, T], BF16)
                        DD = sb(hs, "DD", [128, T])
                        EB = sb(hs, "EB", [128, T], BF16)
                        QF = sb(hs, "QF", [128, T], BF16)
                        QR = sb(hs, "QR", [128, T], BF16)
                        KA = sb(hs, "KA", [128, T], BF16)
                        KB = sb(hs, "KB", [128, T], BF16)
                        DEC = sb(hs, "DEC", [128, NCH])
                        AL = sb(hs, "AL", [128, NCH])
                        BE = sb(hs, "BE", [128, NCH])
                        SR = sb(hs, "SR", [128, 128])
                        SAL = [sb(hs, f"SAL{i}", [128, 128], BF16) for i in range(2)]
                        KXT = [sb(hs, f"KXT{i}", [CH, 128], BF16) for i in range(2)]
                        UT = sb(hs, "UT", [128, 128])
                        ATB = [sb(hs, f"ATB{i}", [CH, CH], BF16) for i in range(4)]
                        for hh in range(0 if 'nohgrn' in flags else 4):
                            base = hh * 640

                            def v3(t_, lo=0, hi=None):
                                v = t_[:, 0:T].rearrange("p (n c) -> p n c", c=CH)
                                return v if hi is None else v[:, :, lo:hi]

                            def dir_process(d):
                                V(lambda e: e.memset(CE[:, 0:1], 0.0), [], ["CE"])
                                V(lambda e: e.tensor_tensor_scan(out=CE[:, 1:T + 1], data0=ONEF[:, 0:T],
                                                                 data1=CE[:, 1:T + 1], initial=0.0,
                                                                 op0=OP.mult, op1=OP.add), ["CE", "ONEF"], ["CE"])
                                L0 = CE[:, 0:T:CH]
                                LM = CE[:, HF:T:CH]
                                L1 = CE[:, CH:T + 1:CH]
                                tt(DEC[:, :], L1, L0, OP.subtract, ["CE"], ["DEC"])
                                act(DEC[:, :], DEC[:, :], AF.Exp, ["DEC"], ["DEC"])
                                lo_, hi_ = (AL, BE) if d == 0 else (BE, AL)
                                tt(lo_[:, :], LM, L0, OP.subtract, ["CE"], ["AB"])
                                tt(hi_[:, :], L1, LM, OP.subtract, ["CE"], ["AB"])
                                act(AL[:, :], AL[:, :], AF.Exp, ["AB"], ["AB"])
                                act(BE[:, :], BE[:, :], AF.Exp, ["AB"], ["AB"])
                                Cv = CE[:, 1:T + 1] if d == 0 else CE[:, 0:T]
                                tt(v3(DD), Cv.rearrange("p (n c) -> p n c", c=CH),
                                   LM.unsqueeze(2).to_broadcast([128, NCH, CH]), OP.subtract, ["CE"], ["DD"])
                                sQ, sK = (1.0, -1.0) if d == 0 else (-1.0, 1.0)
                                act(EB[:, 0:T], DD[:, 0:T], AF.Exp, ["DD"], ["EB"], scale=sQ)
                                tt(QF[:, 0:T], Q[:, 0:T], EB[:, 0:T], OP.mult, ["Q", "EB"], ["QF"])
                                rl, rh = (HF, CH) if d == 0 else (0, HF)
                                al, ah = (0, HF) if d == 0 else (HF, CH)
                                bl, bh = (HF, CH) if d == 0 else (0, HF)
                                S.op("pool", lambda e: e.memset(QR[:, 0:T], 0.0), [], ["QR"])
                                S.op("pool", lambda e: e.memset(KA[:, 0:T], 0.0), [], ["KA"])
                                S.op("pool", lambda e: e.memset(KB[:, 0:T], 0.0), [], ["KB"])
                                V(lambda e: e.tensor_copy(out=v3(QR, rl, rh), in_=v3(QF, rl, rh)), ["QF", "QR"], ["QR"])
                                act(EB[:, 0:T], DD[:, 0:T], AF.Exp, ["DD"], ["EB"], scale=sK)
                                tt(v3(KA, al, ah), v3(KK, al, ah), v3(EB, al, ah), OP.mult, ["KK", "EB", "KA"], ["KA"])
                                tt(v3(KB, bl, bh), v3(KK, bl, bh), v3(EB, bl, bh), OP.mult, ["KK", "EB", "KB"], ["KB"])
                                tt(v3(EB), v3(EB), BE[:, :].unsqueeze(2).to_broadcast([128, NCH, CH]), OP.mult, ["EB", "AB"], ["EB"])
                                tt(EB[:, 0:T], KK[:, 0:T], EB[:, 0:T], OP.mult, ["KK", "EB"], ["EB"])
                                MASK = MASKF if d == 0 else MASKB
                                order, seg_first, seg_last = [], {}, {}
                                for (s0_, sl_, px_) in (segs if d == 0 else segs[::-1]):
                                    cs_ = list(range(s0_ // CH, (s0_ + sl_) // CH))
                                    if d == 1:
                                        cs_ = cs_[::-1]
                                    seg_first[len(order)] = px_
                                    order.extend(cs_)
                                    seg_last[len(order) - 1] = px_
                                TBK = [(PT, "PT"), (PS_, "PS")]
                                SBK = [(PA[0], "PA0"), (PA[1], "PA1")]
                                UBK = [(PA[2], "PA2"), (PA[3], "PA3")]

                                def stage_ab(it):
                                    n = order[it]
                                    a = n * CH
                                    s2 = it % 2
                                    tb, tk = TBK[s2]
                                    mm(tb[0:CH, 0:128], EB[:, a:a + CH], IDB[:, :], True, True, ["EB", "IDB"], [tk], True)
                                    act(KXT[s2][:, :], tb[0:CH, 0:128], AF.Copy, [tk], [f"KXT{s2}"])
                                    sbk, sk = SBK[s2]
                                    mm(sbk[0:CH, 0:CH], KA[:, a:a + CH], QF[:, a:a + CH], True, False, ["KA", "QF"], [sk], False)
                                    mm(sbk[0:CH, 0:CH], KB[:, a:a + CH], QR[:, a:a + CH], False, True, ["KB", "QR"], [sk], True)
                                    tt(ATB[it % 4][:, :], sbk[0:CH, 0:CH], MASK, OP.mult, [sk, "CST"], [f"ATB{it % 4}"])
                                stage_ab(0)
                                for it, n in enumerate(order):
                                    a = n * CH
                                    s2 = it % 2
                                    s4 = it % 4
                                    if it + 1 < NCH:
                                        stage_ab(it + 1)
                                    if it in seg_first:
                                        if vi == 1:
                                            S.dma("sp", SR[:, :], st0[l, d, hh, :, :], w=["SR"], sem="ls0")
                                        else:
                                            V(lambda e: e.memset(SR[:, :], 0.0), [], ["SR"])
                                    ub, uk = UBK[s2]
                                    mm(ub[:, 0:128], KXT[s2][:, :], V32[:, n, :], True, True, [f"KXT{s2}", "V32"], [uk], True)
                                    ts(SAL[s2][:, :], SR[:, :], AL[:, n:n + 1], None, OP.mult, OP.bypass, ["SR", "AB"], [f"SAL{s2}"])
                                    stt(SR[:, :], SR[:, :], DEC[:, n:n + 1], ub[:, 0:128], OP.mult, OP.add, ["SR", "DEC", uk], ["SR"])
                                    pcol = (n % NPB) * CH
                                    mm(PO[:, pcol:pcol + CH], V32[:, n, :], ATB[s4][:, :], True, False, ["V32", f"ATB{s4}"], ["PO"], False)
                                    mm(PO[:, pcol:pcol + CH], SAL[s2][:, :], QF[:, a:a + CH], False, True, [f"SAL{s2}", "QF"], ["PO"], True)
                                    blk_done = (n % NPB == NPB - 1 or n == NCH - 1) if d == 0 else (n % NPB == 0)
                                    if blk_done:
                                        b0 = (n // NPB) * NPB * CH
                                        b1 = min(T, b0 + NPB * CH)
                                        if d == 0:
                                            act(O[:, b0:b1], PO[:, 0:b1 - b0], AF.Copy, ["PO"], ["O"])
                                        else:
                                            tt(O[:, b0:b1], PO[:, 0:b1 - b0], O[:, b0:b1], OP.add, ["PO", "O"], ["O"])
                                    if seg_last.get(it, -1) >= 0:
                                        sname = f"so{d}"
                                        S.dma("sp", nst[seg_last[it], l, d, hh, :, :], SR[:, :], r=["SR"], sem=sname)
                                        if sname not in out_sems:
                                            out_sems.append(sname)

                            def ev_qg(f, t0, tn, pa, pkey):
                                if f == 0:
                                    act(Q[:, t0:t0 + tn], pa, AF.Copy, [pkey], ["Q"])
                                else:
                                    act(SG[:, t0:t0 + tn], pa, AF.Silu, [pkey], ["SG"])

                            def ev_gate(f, t0, tn, pa, pkey):
                                col = l * 8 + f * 4 + hh
                                tm, tkey = next_tm()
                                act(tm[:, 0:tn], pa, AF.Sigmoid, [pkey], [tkey])
                                ts(tm[:, 0:tn], tm[:, 0:tn], OML[:, col:col + 1], LB[:, col:col + 1], OP.mult, OP.add,
                                   [tkey, "LB", "OML"], [tkey])
                                act(CE[:, 1 + t0:1 + t0 + tn], tm[:, 0:tn], AF.Ln, [tkey], ["CE"])
                                act(KK[:, t0:t0 + tn], tm[:, 0:tn], AF.Identity, [tkey], ["KK"],
                                    bias=ONEC[:, 0:1], scale=-1.0)
                                if t0 + tn == T:
                                    dir_process(f)
                            proj_fm(W_in, 0, 16, base + 256, 256, rhsH, T, ev_qg)
                            def ev_v(f, t0, tn, pa, pkey):
                                act(VF[:, t0:t0 + tn], pa, AF.Copy, [pkey], ["VF"])
                                if t0 + tn < T:
                                    return
                                for g4 in range(NCH // 4):
                                    bk, bkey = next_pa()
                                    for j in range(4):
                                        a = (g4 * 4 + j) * CH
                                        mm(bk[0:CH, j * 128:(j + 1) * 128], VF[:, a:a + CH], IDB[:, :], True, True,
                                           ["VF", "IDB"], [bkey], j == 3)
                                    act(V32[:, g4 * 4:g4 * 4 + 4, :], bk[0:CH, 0:512].rearrange("p (j f) -> p j f", f=128),
                                        AF.Copy, [bkey], ["V32"])
                            proj_fm(W_in, 0, 16, base + 512, 128, rhsH, T, ev_v)
                            proj_fm(W_in, 0, 16, base, 256, rhsH, T, ev_gate)
                            for (t0, tn) in TB:
                                sumsq_bcast(lambda c: (O[:, t0:t0 + tn], "O"), 1, t0, tn)
                                rstd_from(PS_[:, 0:tn], "PS", tn, 128)
                                tm, tkey = next_tm()
                                stt(tm[:, 0:tn], O[:, t0:t0 + tn], VEC[:, vb + V_HN + hh:vb + V_HN + hh + 1], RS[:, 0:tn],
                                    OP.mult, OP.mult, ["O", "VEC", "RS"], [tkey])
                                tt(YC[:, hh, t0:t0 + tn], tm[:, 0:tn], SG[:, t0:t0 + tn], OP.mult, [tkey, "SG"], ["YC"])
                        if big:
                            S.barrier()

                    with (ExitStack() if big else nullcontext(ms)) as bs:
                        NT = T // 128
                        GU = sb(bs, "GU", [128, 4, T], BF16)
                        VN = sb(bs, "VN", [128, NT, 512], BF16)
                        XS = sb(bs, "XS", [128, 512])
                        X2 = sb(bs, "X2", [128, 512])
                        SS1 = sb(bs, "SS1", [128, 1])
                        SGB = sb(bs, "SGB", [128, 512])
                        BSB = sb(bs, "BSB", [128, 512])
                        WST = sb(bs, "WST", [128, 4, 128], BF16)
                        S.dma("sp", SGB[:, :], sgn[l].partition_broadcast(128), w=["SGB"], sem="l0")
                        S.dma("sp", BSB[:, :], bsg[l].partition_broadcast(128), w=["BSB"], sem="l1")
                        S.dma("pool", WST[:, :, :], wsgT[l].rearrange("h s p -> s h p"), w=["WST"], sem="l3")

                        def gelu_to(dst, dkey, pa, pkey, n):
                            act(dst, pa, AF.Gelu_apprx_tanh, [pkey], [dkey])

                        def ev_u(off):
                            def ev(f, t0, tn, pa, pkey):
                                gelu_to(GU[:, off + f, t0:t0 + tn], "GU", pa, pkey, tn)
                            return ev
                        cb = 2560
                        proj_fm(W_in, 0, 16, cb, 256, rhsH, T, ev_u(0))
                        proj_fm(W_in, 0, 16, cb + 256, 256, rhsH, T, ev_u(2))
                        wv0, wk0 = load_w(W_in, 0, 16, cb + 512, 256)
                        wv1, wk1 = load_w(W_in, 0, 16, cb + 768, 256)
                        for n in range(NT):
                            pa, pkey = next_pa()
                            for hf, (wv, wk) in enumerate(((wv0, wk0), (wv1, wk1))):
                                for j in range(2):
                                    cj = hf * 2 + j
                                    for k in range(16):
                                        mm(pa[:, cj * 128:(cj + 1) * 128], H[:, k, n * 128:(n + 1) * 128], wv[:, j, k, :],
                                           k == 0, k == 15, [wk, "H"], [pkey], k == 15)
                            tm, tkey = next_tm()
                            gelu_to(tm[:, :], tkey, pa[:, :], pkey, 512)
                            act(X2[:, :], tm[:, :], AF.Square, [tkey], ["X2"])
                            V(lambda e: e.tensor_reduce(out=SS1[:, 0:1], in_=X2[:, :], axis=mybir.AxisListType.X, op=OP.add),
                              ["X2"], ["SS1"])
                            act(SS1[:, :], SS1[:, :], AF.Ln, ["SS1"], ["SS1"], bias=EPSC[:, 0:1], scale=1.0 / 512)
                            act(SS1[:, :], SS1[:, :], AF.Exp, ["SS1"], ["SS1"], scale=-0.5)
                            stt(VN[:, n, :], tm[:, :], SS1[:, 0:1], SGB[:, :], OP.mult, OP.mult, [tkey, "SS1", "SGB"], ["VN"])
                        for hb in range(4):
                            for n in range(NT):
                                mm(PT[:, 0:128], VN[:, n, hb * 128:(hb + 1) * 128], WST[:, hb, :], True, True,
                                   ["VN", "WST"], ["PTsgu"], True)
                                tm, tkey = next_tm()
                                tt(tm[:, 0:128], PT[:, 0:128], BSB[:, hb * 128:(hb + 1) * 128], OP.add, ["PTsgu", "BSB"], [tkey])
                                tt(YC[:, 4 + hb, n * 128:(n + 1) * 128], tm[:, 0:128], GU[:, hb, n * 128:(n + 1) * 128], OP.mult,
                                   [tkey, "GU"], ["YC"])
                        if big:
                            S.barrier()

                    with ExitStack() as cs:
                        CC = sb(cs, "CC", [128, 4, T])
                        CY = sb(cs, "CY", [128, 4, T])
                        cb = 3584

                        def ev_cc(off):
                            def ev(f, t0, tn, pa, pkey):
                                act(CC[:, off + f, t0:t0 + tn], pa, AF.Copy, [pkey], [f"CC{off + f}"])
                            return ev

                        def ev_ch(off):
                            def ev(f, t0, tn, pa, pkey):
                                j = off + f
                                tt(CC[:, j, t0:t0 + tn], pa, CC[:, j, t0:t0 + tn], OP.mult, [pkey, f"CC{j}"], [f"CC{j}"])
                                if t0 + tn == T:
                                    w0 = VEC[:, vb + V_WC + 0 * 4 + j:vb + V_WC + 0 * 4 + j + 1]
                                    w1 = VEC[:, vb + V_WC + 1 * 4 + j:vb + V_WC + 1 * 4 + j + 1]
                                    w2 = VEC[:, vb + V_WC + 2 * 4 + j:vb + V_WC + 2 * 4 + j + 1]
                                    act(CY[:, j, 0:T], CC[:, j, 0:T], AF.Copy, [f"CC{j}"], [f"CY{j}"], scale=w1)
                                    for (s0_, sl_, _px) in segs:
                                        e0_ = s0_ + sl_
                                        stt(CY[:, j, s0_ + 1:e0_], CC[:, j, s0_:e0_ - 1], w0, CY[:, j, s0_ + 1:e0_], OP.mult, OP.add, [f"CC{j}", f"CY{j}"], [f"CY{j}"])
                                        stt(CY[:, j, s0_:e0_ - 1], CC[:, j, s0_ + 1:e0_], w2, CY[:, j, s0_:e0_ - 1], OP.mult, OP.add, [f"CC{j}", f"CY{j}"], [f"CY{j}"])
                            return ev

                        def ev_cb(off):
                            def ev(f, t0, tn, pa, pkey):
                                j = off + f
                                tt(YC[:, 8 + j, t0:t0 + tn], pa, CY[:, j, t0:t0 + tn], OP.mult, [pkey, f"CY{j}"], ["YC"])
                            return ev
                        proj_fm(W_in, 0, 16, cb + 512, 256, rhsH, T, ev_cc(0))
                        proj_fm(W_in, 0, 16, cb + 768, 256, rhsH, T, ev_cc(2))
                        proj_fm(W_in, 0, 16, cb + 1024, 256, rhsH, T, ev_ch(0))
                        proj_fm(W_in, 0, 16, cb + 1280, 256, rhsH, T, ev_ch(2))
                        proj_fm(W_in, 0, 16, cb, 256, rhsH, T, ev_cb(0))
                        proj_fm(W_in, 0, 16, cb + 256, 256, rhsH, T, ev_cb(2))
                        S.barrier()

                    with ExitStack() as dsx:
                        DX = sb(dsx, "DX", [128, T])
                        CS = sb(dsx, "CS", [128, T + 1])
                        SM = sb(dsx, "SM", [128, T])
                        DF = sb(dsx, "DF", [128, T], BF16)
                        WP = sb(dsx, "WP", [128, 4, 128], BF16)
                        S.dma("pool", WP[:, :, :], wpool[l].rearrange("g i o -> i g o"), w=["WP"], sem="l3")
                        cb = 5120

                        def ev_d(off):
                            def ev(f, t0, tn, pa, pkey):
                                gi = off + f
                                win = POOLW[gi]
                                hw = win // 2
                                act(DX[:, t0:t0 + tn], pa, AF.Copy, [pkey], ["DX"])
                                if t0 + tn < T:
                                    return
                                for (s0_, L_, _px) in segs:
                                    V(lambda e: e.memset(CS[:, 0:1], 0.0), [], ["CS"])
                                    V(lambda e: e.tensor_tensor_scan(out=CS[:, 1:L_ + 1], data0=ONEF[:, 0:L_], data1=DX[:, s0_:s0_ + L_],
                                                                     initial=0.0, op0=OP.mult, op1=OP.add), ["DX", "ONEF"], ["CS"])
                                    SMs = SM[:, s0_:s0_ + L_]
                                    tt(SMs[:, hw:L_ - hw + 1], CS[:, 2 * hw:L_ + 1], CS[:, 0:L_ - 2 * hw + 1], OP.subtract, ["CS"], ["SM"])
                                    V(lambda e: e.tensor_copy(out=SMs[:, 0:hw], in_=CS[:, hw:2 * hw]), ["CS"], ["SM"])
                                    if hw > 1:
                                        ts(SMs[:, L_ - hw + 1:L_], CS[:, L_ - 2 * hw + 1:L_ - hw], -1.0, CS[:, L_:L_ + 1], OP.mult, OP.add, ["CS"], ["SM"])
                                    ts(SMs[:, hw:L_ - hw + 1], SMs[:, hw:L_ - hw + 1], 1.0 / win, None, OP.mult, OP.bypass, ["SM"], ["SM"])
                                    for t in range(hw):
                                        ts(SMs[:, t:t + 1], SMs[:, t:t + 1], 1.0 / (t + hw), None, OP.mult, OP.bypass, ["SM"], ["SM"])
                                    for t in range(L_ - hw + 1, L_):
                                        ts(SMs[:, t:t + 1], SMs[:, t:t + 1], 1.0 / (L_ - t + hw), None, OP.mult, OP.bypass, ["SM"], ["SM"])
                                tt(DF[:, 0:T], SM[:, 0:T], DX[:, 0:T], OP.subtract, ["SM", "DX"], ["DF"])
                                for (u0, un) in TB:
                                    pa2, pk2 = next_pa()
                                    mm(pa2[:, 0:un], WP[:, gi, :], DF[:, u0:u0 + un], True, True, ["WP", "DF"], [pk2], True)
                                    act(YC[:, 12 + gi, u0:u0 + un], pa2[:, 0:un], AF.Copy, [pk2, "VEC"], ["YC"],
                                        scale=VEC[:, vb + V_PS + gi:vb + V_PS + gi + 1])
                            return ev
                        proj_fm(W_in, 0, 16, cb, 256, rhsH, T, ev_d(0))
                        proj_fm(W_in, 0, 16, cb + 256, 256, rhsH, T, ev_d(2))
                        if big:
                            S.barrier()

                    def rhsY(k, t0, tn):
                        return YC[:, k, t0:t0 + tn], "YC"

                    def ev_out(db):
                        def ev(f, t0, tn, pa, pkey):
                            dch = db * 2 + f
                            stt(X[:, dch, t0:t0 + tn], pa, MOD[l][:, 32 + dch, vi:vi + 1], X[:, dch, t0:t0 + tn],
                                OP.mult, OP.add, [pkey, f"MOD{l}_2", f"X{dch}"], [f"X{dch}"])
                        return ev
                    bg_need(l, 48)
                    for db in range(8):
                        proj_fm(w_out[l], 0, 16, db * 256, 256, rhsY, T, ev_out(db))
                    S.barrier()

                bg_need(l, 80)
                norm_to_H(A2[l], lambda c: MOD[l][:, 48 + c, vi:vi + 1], T, vi, [f"A2_{l}", f"MOD{l}_3"])
                qoff = 0
                with ExitStack() as fs:
                  G = sb(fs, "G", [128, 12, T], BF16)
                  CVV = sb(fs, "CVV", [128, 2, T])
                  CVG = [sb(fs, f"CVG{i}", [128, T]) for i in range(2)]
                  cvg_i = [0]
                  for nq in (() if 'noffn' in flags else (12, 12, 10, 10)):
                    if True:

                        def ev_up(is_gate, i0):
                            pend = []

                            def ev(f, t0, tn, pa, pkey):
                                i = i0 + f
                                ch = (44 if is_gate else 0) + qoff + i
                                w0 = VEC[:, vb + V_WCF + 0 * 88 + ch:vb + V_WCF + 0 * 88 + ch + 1]
                                w1 = VEC[:, vb + V_WCF + 1 * 88 + ch:vb + V_WCF + 1 * 88 + ch + 1]
                                w2 = VEC[:, vb + V_WCF + 2 * 88 + ch:vb + V_WCF + 2 * 88 + ch + 1]
                                if is_gate:
                                    if t0 == 0:
                                        cvg_i[0] += 1
                                    gi_ = cvg_i[0] % 2
                                    cvt, ck = CVG[gi_][:, :], f"CVG{gi_}"
                                else:
                                    cvt, ck = CVV[:, f, :], f"CVV{f}"
                                act(cvt[:, t0:t0 + tn], pa, AF.Copy, [pkey, "VEC"], [ck], scale=w1)
                                pend.append((t0, tn, pa, pkey))
                                if t0 + tn < T:
                                    return
                                for (s0_, sl_, _px) in segs:
                                    e0_ = s0_ + sl_
                                    for bi, (b0, bn, pb, pbk) in enumerate(pend):
                                        lo, hi = max(s0_, b0), min(e0_, b0 + bn)
                                        if hi - lo < 2:
                                            continue
                                        stt(cvt[:, lo + 1:hi], pb[:, lo - b0:hi - 1 - b0], w0, cvt[:, lo + 1:hi], OP.mult, OP.add, [pbk, ck], [ck])
                                        stt(cvt[:, lo:hi - 1], pb[:, lo + 1 - b0:hi - b0], w2, cvt[:, lo:hi - 1], OP.mult, OP.add, [pbk, ck], [ck])
                                        if bi + 1 < len(pend) and hi < e0_ and hi == b0 + bn:
                                            nb0, nbn, npb, npbk = pend[bi + 1]
                                            stt(cvt[:, hi:hi + 1], pb[:, bn - 1:bn], w0, cvt[:, hi:hi + 1], OP.mult, OP.add, [pbk, ck], [ck])
                                            stt(cvt[:, hi - 1:hi], npb[:, 0:1], w2, cvt[:, hi - 1:hi], OP.mult, OP.add, [npbk, ck], [ck])
                                pend.clear()
                                if is_gate:
                                    act(cvt[:, 0:T], cvt[:, 0:T], AF.Silu, [ck], [ck])
                                    tt(G[:, i, 0:T], CVV[:, f, 0:T], cvt[:, 0:T], OP.mult, [f"CVV{f}", ck], ["G"])
                            return ev
                        for i0 in range(0, nq, 2):
                            proj_fm(w_up[l], 0, 16, (qoff + i0) * 128, 256, rhsH, T, ev_up(False, i0))
                            proj_fm(w_up[l], 0, 16, DFF + (qoff + i0) * 128, 256, rhsH, T, ev_up(True, i0))

                        def rhsG(k, t0, tn):
                            return G[:, k, t0:t0 + tn], "G"

                        def ev_dn(db):
                            def ev(f, t0, tn, pa, pkey):
                                dch = db * 2 + f
                                stt(X[:, dch, t0:t0 + tn], pa, MOD[l][:, 80 + dch, vi:vi + 1], X[:, dch, t0:t0 + tn],
                                    OP.mult, OP.add, [pkey, f"MOD{l}_5", f"X{dch}"], [f"X{dch}"])
                            return ev
                        bg_need(l, 96)
                        for db in range(8):
                            proj_fm(w_down[l], qoff, nq, db * 256, 256, rhsG, T, ev_dn(db))
                    qoff += nq
                  S.barrier()

            yTv = yT.rearrange("(c p) t -> p c t", p=128)
            for (t0, tn) in TB:
                sumsq_bcast(lambda c: (X[:, c, t0:t0 + tn], f"X{c}"), 16, t0, tn)
                rstd_from(PS_[:, 0:tn], "PS", tn, D)
                for c in range(16):
                    tm, tkey = next_tm()
                    stt(tm[:, 0:tn], X[:, c, t0:t0 + tn], VEC[:, V_FIN + c:V_FIN + c + 1], RS[:, 0:tn], OP.mult, OP.mult,
                        [f"X{c}", "VEC", "RS"], [tkey])
                    sname = f"sy{c % 2}"
                    S.dma("sp", yTv[:, c, tok0 + t0:tok0 + t0 + tn], tm[:, 0:tn], r=[tkey], sem=sname)
                    if sname not in out_sems:
                        out_sems.append(sname)
            S.barrier()
            gscope.close()

        need = {s_: S.dsem[s_][1] for s_ in out_sems}
        S._wait("sp", need)
    return nc


def _fm(v):
    v = np.asarray(v, np.float32)
    return np.ascontiguousarray(v.reshape(-1, 128).T)


def _pos_embed_T():
    rows = TS // 64
    r = np.repeat(np.arange(rows), 64).astype(np.float32)[:, None]
    col = np.tile(np.arange(64), rows).astype(np.float32)[:, None]
    quarter = D // 4
    freq = np.exp(np.float32(-np.log(10000.0)) * np.arange(quarter, dtype=np.float32) / np.float32(quarter)).astype(np.float32)[None, :]
    emb = np.concatenate([np.sin(r * freq), np.cos(r * freq), np.sin(col * freq), np.cos(col * freq)], -1).astype(np.float32)
    return np.ascontiguousarray(emb.T)


_PROG = {}


def kernel(x_prompt, x_sample, c, state_hgrn, c_ctx, w_ada, b_ada, norm_mix, norm_ffn, w_in,
           lb_logits, hgrn_norm, sgu_norm, w_sgu, b_sgu, w_conv_c, w_pool, pool_scale, w_out,
           w_up, w_conv_ffn, w_down, norm_final):
    f = lambda a: np.asarray(a, np.float32)
    x_prompt, x_sample, c, state_hgrn, c_ctx = f(x_prompt), f(x_sample), f(c), f(state_hgrn), f(c_ctx)
    w_ada, w_in, w_out, w_up, w_down = f(w_ada), f(w_in), f(w_out), f(w_up), f(w_down)
    vecs = np.zeros((128, NV), np.float32)
    for l in range(DEPTH):
        vb = l * V_PER
        vecs[:, vb + V_BADA:vb + V_BADA + 96] = _fm(b_ada[l])
        vecs[:, vb + V_NM:vb + V_NM + 16] = _fm(norm_mix[l])
        vecs[:, vb + V_NF:vb + V_NF + 16] = _fm(norm_ffn[l])
        vecs[:, vb + V_HN:vb + V_HN + 4] = _fm(hgrn_norm[l])
        wc = f(w_conv_c)[l]
        for tap in range(3):
            vecs[:, vb + V_WC + tap * 4:vb + V_WC + tap * 4 + 4] = _fm(wc[tap])
        vecs[:, vb + V_PS:vb + V_PS + 4] = _fm(pool_scale[l])
        wcf = f(w_conv_ffn)[l]
        for tap in range(3):
            vecs[:, vb + V_WCF + tap * 88:vb + V_WCF + tap * 88 + 88] = _fm(wcf[tap])
        lbl = f(lb_logits)[l]
        for d in range(2):
            vecs[:, vb + V_LBL + d * 4:vb + V_LBL + d * 4 + 4] = _fm(lbl[d])
    vecs[:, V_FIN:V_FIN + 16] = _fm(norm_final)
    consts = np.zeros((128, 384), np.float32)
    consts[:, 0:128] = np.eye(128, dtype=np.float32)
    ii = np.arange(64)
    consts[0:64, 128:192] = (ii[None, :] >= ii[:, None]).astype(np.float32)
    consts[0:64, 256:320] = (ii[:, None] >= ii[None, :]).astype(np.float32)
    perm = []
    for h in range(4):
        for grp in (2, 3, 0, 4, 1):
            perm.extend(range(grp * 512 + h * 128, grp * 512 + (h + 1) * 128))
    perm.extend(range(2560, 5632))
    def tile_w(w):
        L, K, N = w.shape
        return np.ascontiguousarray(w.reshape(L, K // 128, 128, N // 128, 128).transpose(0, 3, 2, 1, 4)).reshape(L, N // 128, 128, K)
    w_in_p = tile_w(w_in[:, :, np.asarray(perm)])
    w_ada, w_out, w_up, w_down = tile_w(w_ada), tile_w(w_out), tile_w(w_up), tile_w(w_down)
    wsgT = np.ascontiguousarray(np.transpose(f(w_sgu), (0, 1, 3, 2)))
    posT = _pos_embed_T()
    sgn = np.ascontiguousarray(f(sgu_norm))
    bsg = np.ascontiguousarray(f(b_sgu).reshape(DEPTH, 512))
    wpl = np.ascontiguousarray(f(w_pool))

    in_maps = []
    for core in range(NCORES):
        b = core % 2
        xT = np.ascontiguousarray(np.concatenate(
            [x_prompt[2 * core].T, x_prompt[2 * core + 1].T, x_sample[b].T], axis=1))
        cvv = np.stack([_fm(c_ctx), _fm(c[b])], axis=-1)
        in_maps.append(dict(xT=xT, posT=posT, cv=np.ascontiguousarray(cvv), st0=np.ascontiguousarray(state_hgrn[b]),
                            vecs=vecs, sgn=sgn, bsg=bsg, wsgT=wsgT, wpool=wpl, consts=consts,
                            w_ada=w_ada, w_in=w_in_p, w_out=w_out, w_up=w_up, w_down=w_down))
    if "nc" not in _PROG:
        _PROG["nc"] = build_program()
    res = run_bass_kernel_spmd(_PROG["nc"], in_maps, core_ids=list(range(NCORES)))
    y_prompt = np.zeros((16, TP, D), np.float32)
    y_sample = np.zeros((2, TS, D), np.float32)
    new_state = np.zeros((16, DEPTH, 2, 4, 128, 128), np.float32)
    for core in range(NCORES):
        r = res.results[core]
        yT = r["yT"]
        y_prompt[2 * core] = yT[:, 0:TP].T
        y_prompt[2 * core + 1] = yT[:, TP:2 * TP].T
        if core < 2:
            y_sample[core] = yT[:, 2 * TP:].T
        new_state[2 * core] = r["nst"][0]
        new_state[2 * core + 1] = r["nst"][1]
    return (y_prompt, y_sample, new_state)
```

```python
from contextlib import ExitStack, nullcontext
import numpy as np
import concourse.bass as bass
import concourse.mybir as mybir
from concourse.bass_utils import run_bass_kernel_spmd

F32 = mybir.dt.float32
BF16 = mybir.dt.bfloat16
AF = mybir.ActivationFunctionType
OP = mybir.AluOpType

D = 2048
DEPTH = 2
DFF = 5632
NCORES = 8
TP = 256
TS = 1024
TTOT = 2 * TP + TS
EPS = 1e-6
POOLW = (2, 4, 8, 16)
V_BADA, V_NM, V_NF, V_HN, V_WC, V_PS, V_WCF, V_LBL = 0, 96, 112, 128, 132, 144, 148, 412
V_PER = 420
V_FIN = 2 * V_PER
NV = V_FIN + 16


class Sch:
    def __init__(self, nc, st):
        self.nc = nc
        self.st = st
        self.E = {}
        for n, eng in [("pe", nc.tensor), ("dve", nc.vector), ("act", nc.scalar),
                       ("pool", nc.gpsimd), ("sp", nc.sync)]:
            sem = st.enter_context(nc.semaphore("s_" + n))
            self.E[n] = dict(eng=eng, sem=sem, cnt=0, waited={})
        self.dsem = {}
        self.lw = {}
        self.rd = {}

    def semof(self, src):
        if src in self.E:
            return self.E[src]["sem"]
        return self.dsem[src][0]

    def _wait(self, en, need):
        e = self.E[en]
        for src, val in need.items():
            if src == en and en == "pe":
                continue
            if e["waited"].get(src, 0) < val:
                e["eng"].wait_ge(self.semof(src), val)
                e["waited"][src] = val

    def _deps(self, r, w):
        need = {}

        def add(t):
            if t is not None:
                need[t[0]] = max(need.get(t[0], 0), t[1])
        for k in r:
            add(self.lw.get(k))
        for k in w:
            add(self.lw.get(k))
            for s_, v_ in self.rd.get(k, {}).items():
                add((s_, v_))
        return need

    def _reg(self, src, tval, r, w):
        for k in r:
            d = self.rd.setdefault(k, {})
            d[src] = max(d.get(src, 0), tval)
        for k in w:
            self.lw[k] = (src, tval)
            self.rd[k] = {}

    def op(self, en, fn, r=(), w=(), inc=True):
        e = self.E[en]
        self._wait(en, self._deps(r, w))
        ins = fn(e["eng"])
        if inc:
            e["cnt"] += 1
            ins.then_inc(e["sem"], 1)
            tval = e["cnt"]
        else:
            tval = e["cnt"] + 1
        self._reg(en, tval, r, w)

    def dma(self, q, out, in_, r=(), w=(), sem="d0", accum=False):
        if sem not in self.dsem:
            self.dsem[sem] = [self.st.enter_context(self.nc.semaphore("dq_" + sem)), 0]
        ds = self.dsem[sem]
        need = self._deps(r, w)
        if ds[1] > 0:
            need[sem] = max(need.get(sem, 0), ds[1])
        self._wait(q, need)
        if accum:
            self.E[q]["eng"].dma_start(out=out, in_=in_, accum_op=OP.add).then_inc(ds[0], 16)
        else:
            self.E[q]["eng"].dma_start(out=out, in_=in_).then_inc(ds[0], 16)
        ds[1] += 16
        self._reg(sem, ds[1], r, w)

    def barrier(self):
        for en in self.E:
            need = {o: self.E[o]["cnt"] for o in self.E if self.E[o]["cnt"] > 0}
            for s_, d_ in self.dsem.items():
                if d_[1] > 0:
                    need[s_] = d_[1]
            self._wait(en, need)
        self.lw = {}
        self.rd = {}


def tblocks(T):
    return [(t0, min(512, T - t0)) for t0 in range(0, T, 512)]


def build_program(flags=()):
    nc = bass.Bass("TRN2", target_bir_lowering=False)

    def din(name, shape):
        return nc.dram_tensor(name, list(shape), F32, kind="ExternalInput").ap()
    xT = din("xT", [D, TTOT])
    posT = din("posT", [D, TS])
    cv = din("cv", [128, 16, 2])
    st0 = din("st0", [DEPTH, 2, 4, 128, 128])
    vecs = din("vecs", [128, NV])
    sgn = din("sgn", [DEPTH, 512])
    bsg = din("bsg", [DEPTH, 512])
    wsgT = din("wsgT", [DEPTH, 4, 128, 128])
    wpool = din("wpool", [DEPTH, 4, 128, 128])
    consts = din("consts", [128, 384])
    w_ada = din("w_ada", [DEPTH, 96, 128, 16 * 128])
    w_in = din("w_in", [DEPTH, 44, 128, 16 * 128])
    w_out = din("w_out", [DEPTH, 16, 128, 16 * 128])
    w_up = din("w_up", [DEPTH, 88, 128, 16 * 128])
    w_down = din("w_down", [DEPTH, 16, 128, 44 * 128])
    yT = nc.dram_tensor("yT", [D, TTOT], F32, kind="ExternalOutput").ap()
    nst = nc.dram_tensor("nst", [2, DEPTH, 2, 4, 128, 128], F32, kind="ExternalOutput").ap()

    uid = [0]

    with ExitStack() as st:
        S = Sch(nc, st)

        def sb(stack, name, shape, dt=F32):
            uid[0] += 1
            return stack.enter_context(nc.sbuf_tensor(f"{name}_{uid[0]}", list(shape), dt))

        def ps(name, shape, dt=F32):
            return st.enter_context(nc.psum_tensor(name, list(shape), dt))

        X = sb(st, "X", [128, 16, TS])
        H = sb(st, "H", [128, 16, TS], BF16)
        NW = 2
        WB = [sb(st, f"WB{i}", [128, 4096], BF16) for i in range(NW)]
        VEC = sb(st, "VEC", [128, NV])
        MOD = [sb(st, f"MOD{l}", [128, 96, 2]) for l in range(DEPTH)]
        A1 = [sb(st, f"A1{l}", [128, 16, 2]) for l in range(DEPTH)]
        A2 = [sb(st, f"A2{l}", [128, 16, 2]) for l in range(DEPTH)]
        LB = sb(st, "LB", [128, 16])
        OML = sb(st, "OML", [128, 16])
        CST = sb(st, "CST", [128, 384])
        IDB = sb(st, "IDB", [128, 128], BF16)
        ONB = sb(st, "ONB", [128, 128], BF16)
        EPSC = sb(st, "EPSC", [128, 1])
        ONEC = sb(st, "ONEC", [128, 1])
        ONEF = sb(st, "ONEF", [128, TS], BF16)
        SC = sb(st, "SC", [128, 16, 2], BF16)
        CVT = sb(st, "CVT", [128, 16, 2])
        ADR = [sb(st, f"ADR{i}", [2, 256]) for i in range(2)]
        SQ = [sb(st, f"SQ{i}", [128, 512], BF16) for i in range(2)]
        RS = sb(st, "RS", [128, 512])
        RT = sb(st, "RT", [128, 512])
        TM = [sb(st, f"TM{i}", [128, 512]) for i in range(2)]
        MASKF = CST[0:32, 128:160]
        MASKB = CST[0:32, 256:288]

        PA = [ps(f"PA{i}", [128, 512]) for i in range(4)]
        PS_ = ps("PSs", [128, 512])
        PT = ps("PT", [128, 512])
        PTB = ps("PTB", [128, 512])
        PO = ps("PO", [128, 512])
        pa_i = [0]
        sq_i = [0]
        tm_i = [0]
        ws_i = [0]

        def V(fn, r, w):
            S.op("dve", fn, r, w)

        def A(fn, r, w):
            S.op("act", fn, r, w)

        def act(out, in_, func, r, w, bias=None, scale=1.0):
            if bias is None:
                S.op("act", lambda e: e.activation(out=out, in_=in_, func=func, scale=scale), r, w)
            else:
                S.op("act", lambda e: e.activation(out=out, in_=in_, func=func, bias=bias, scale=scale), r, w)

        def tt(out, in0, in1, op, r, w):
            V(lambda e: e.tensor_tensor(out=out, in0=in0, in1=in1, op=op), r, w)

        def ts(out, in0, s1, s2, op0, op1, r, w):
            V(lambda e: e.tensor_scalar(out=out, in0=in0, scalar1=s1, scalar2=s2, op0=op0, op1=op1), r, w)

        def stt(out, in0, scalar, in1, op0, op1, r, w):
            V(lambda e: e.scalar_tensor_tensor(out=out, in0=in0, scalar=scalar, in1=in1, op0=op0, op1=op1), r, w)

        def mm(out, lhsT, rhs, start, stop, r, w, inc):
            S.op("pe", lambda e: e.matmul(out, lhsT=lhsT, rhs=rhs, start=start, stop=stop), r, w, inc=inc)

        def load_w(srct, k0, nK, c0, ncols):
            slot = ws_i[0] % len(WB)
            ws_i[0] += 1
            nch = ncols // 128
            cc0 = c0 // 128
            view = WB[slot][:, 0:nch * nK * 128].rearrange("p (c k f) -> p c k f", c=nch, k=nK)
            srcv = srct[cc0:cc0 + nch, :, k0 * 128:(k0 + nK) * 128].rearrange("c p (k f) -> p c k f", k=nK)
            S.dma("pool", view, srcv, r=[], w=[f"WB{slot}"], sem=f"dw{slot}")
            return view, f"WB{slot}"

        def next_pa():
            i = pa_i[0] % 4
            pa_i[0] += 1
            return PA[i], f"PA{i}"

        def next_tm():
            i = tm_i[0] % 2
            tm_i[0] += 1
            return TM[i], f"TM{i}"

        def proj_fm(srct, k0, nK, c0, ncols, rhs_fn, T, evac):
            wv, wkey = load_w(srct, k0, nK, c0, ncols)
            for f in range(ncols // 128):
                for (t0, tn) in tblocks(T):
                    pa, pkey = next_pa()
                    for k in range(nK):
                        rhs, rkey = rhs_fn(k, t0, tn)
                        mm(pa[:, 0:tn], wv[:, f, k, :], rhs, k == 0, k == nK - 1,
                           [wkey, rkey], [pkey], k == nK - 1)
                    evac(f, t0, tn, pa[:, 0:tn], pkey)
            bg_step()

        def rstd_from(ss_ap, sskey, tn, n):
            act(RT[:, 0:tn], ss_ap, AF.Ln, [sskey], ["RT"], bias=EPSC[:, 0:1], scale=1.0 / n)
            act(RS[:, 0:tn], RT[:, 0:tn], AF.Exp, ["RT"], ["RS"], scale=-0.5)

        def sumsq_bcast(src_fn, nchunks, t0, tn):
            for c in range(nchunks):
                src, skey = src_fn(c)
                i = sq_i[0] % 2
                sq_i[0] += 1
                act(SQ[i][:, 0:tn], src, AF.Square, [skey], [f"SQ{i}"])
                mm(PS_[:, 0:tn], ONB[:, :], SQ[i][:, 0:tn], c == 0, c == nchunks - 1,
                   [f"SQ{i}", "ONB"], ["PS"], True)

        def norm_to_H(Aco, Bco_fn, T, vi, mkeys):
            for (t0, tn) in tblocks(T):
                sumsq_bcast(lambda c: (X[:, c, t0:t0 + tn], f"X{c}"), 16, t0, tn)
                rstd_from(PS_[:, 0:tn], "PS", tn, D)
                for c in range(16):
                    tm, tkey = next_tm()
                    tt(tm[:, 0:tn], X[:, c, t0:t0 + tn], RS[:, 0:tn], OP.mult, [f"X{c}", "RS"], [tkey])
                    act(H[:, c, t0:t0 + tn], tm[:, 0:tn], AF.Identity, [tkey] + mkeys, ["H"],
                        bias=Bco_fn(c), scale=Aco[:, c, vi:vi + 1])

        S.dma("sp", VEC[:, :], vecs[:, :], w=["VEC"], sem="l0")
        S.dma("sp", CST[:, :], consts[:, :], w=["CST"], sem="l1")
        S.dma("sp", CVT[:, :, :], cv[:, :, :], w=["CVT"], sem="l2")
        V(lambda e: e.tensor_copy(out=IDB[:, :], in_=CST[:, 0:128]), ["CST"], ["IDB"])
        V(lambda e: e.memset(ONB[:, :], 1.0), [], ["ONB"])
        V(lambda e: e.memset(EPSC[:, :], EPS), [], ["EPSC"])
        V(lambda e: e.memset(ONEC[:, :], 1.0), [], ["ONEC"])
        V(lambda e: e.memset(ONEF[:, :], 1.0), [], ["ONEF"])
        act(SC[:, :, :], CVT[:, :, :], AF.Silu, ["CVT"], ["SC"])
        V(lambda e: e.memset(LB[:, 0:8], 0.0), [], ["LB"])
        tt(LB[:, 8:16], VEC[:, V_PER + V_LBL:V_PER + V_LBL + 8], VEC[:, V_LBL:V_LBL + 8], OP.subtract, ["VEC"], ["LB"])
        act(LB[:, 8:16], LB[:, 8:16], AF.Sigmoid, ["LB"], ["LB"])
        ts(OML[:, :], LB[:, :], -1.0, 1.0, OP.mult, OP.add, ["LB"], ["OML"])

        mod_done = [0, 0]

        ada_pend = []
        adr_i = [0]

        def ada_flush():
            if not ada_pend:
                return
            l, fb, adr, akey = ada_pend.pop()
            vb = l * V_PER
            pb, pbkey = next_pa()
            for f in range(2):
                mm(pb[:, 2 * f:2 * f + 2], adr[0:2, f * 128:(f + 1) * 128], CST[0:2, 0:2], True, True,
                   [akey, "CST"], [pbkey], f == 1)
            for f in range(2):
                ch = fb * 2 + f
                ts(MOD[l][:, ch, :], pb[:, 2 * f:2 * f + 2], VEC[:, vb + V_BADA + ch:vb + V_BADA + ch + 1], None,
                   OP.add, OP.bypass, [pbkey, "VEC"], [f"MOD{l}_{ch // 16}"])
            mod_done[l] = fb * 2 + 2
            if mod_done[l] == 32:
                for vi in range(2):
                    ts(A1[l][:, :, vi], MOD[l][:, 16:32, vi], 1.0, None, OP.add, OP.bypass, [f"MOD{l}_1"], [f"A1_{l}"])
                    tt(A1[l][:, :, vi], A1[l][:, :, vi], VEC[:, vb + V_NM:vb + V_NM + 16], OP.mult, [f"A1_{l}", "VEC"], [f"A1_{l}"])
            if mod_done[l] == 80:
                for vi in range(2):
                    ts(A2[l][:, :, vi], MOD[l][:, 64:80, vi], 1.0, None, OP.add, OP.bypass, [f"MOD{l}_4"], [f"A2_{l}"])
                    tt(A2[l][:, :, vi], A2[l][:, :, vi], VEC[:, vb + V_NF:vb + V_NF + 16], OP.mult, [f"A2_{l}", "VEC"], [f"A2_{l}"])

        def ada_block(l, fb):
            wv, wkey = load_w(w_ada[l], 0, 16, fb * 256, 256)
            pa, pkey = next_pa()
            for k in range(16):
                mm(pa[0:2, 0:256], SC[:, k, :], wv[:, :, k, :], k == 0, k == 15, [wkey, "SC"], [pkey], k == 15)
            i = adr_i[0] % 2
            adr_i[0] += 1
            act(ADR[i][:, :], pa[0:2, 0:256], AF.Copy, [pkey], [f"ADR{i}"])
            ada_flush()
            ada_pend.append((l, fb, ADR[i], f"ADR{i}"))

        def ada_gen():
            for l_ in range(DEPTH):
                for fb_ in range(48):
                    ada_block(l_, fb_)
                    yield
            ada_flush()
            yield

        bg = ada_gen()

        def bg_step():
            next(bg, None)

        def bg_need(l, nchunks):
            while mod_done[l] < nchunks:
                if next(bg, "END") == "END":
                    break

        out_sems = []

        seqs = [(2 * TP, TS, 1, [(0, TS, -1)]), (0, 2 * TP, 0, [(0, TP, 0), (TP, TP, 1)])]
        if 'noprompt' in flags:
            seqs = seqs[:1]
        if 'nosample' in flags:
            seqs = seqs[1:]
        if 'adaonly' in flags:
            seqs = []
        for (tok0, T, vi, segs) in seqs:
            TB = tblocks(T)
            gscope = st.enter_context(ExitStack())
            CH, HF = 32, 16
            NPB = 512 // CH
            NCH = T // CH
            xTv = xT.rearrange("(c p) t -> p c t", p=128)
            for c in range(16):
                S.dma("sp", X[:, c, 0:T], xTv[:, c, tok0:tok0 + T], w=[f"X{c}"], sem=f"lx{c % 4}")
            if vi == 1:
                pv = posT.rearrange("(c p) t -> p c t", p=128)
                for c in range(16):
                    S.dma("pool", X[:, c, 0:T], pv[:, c, 0:T], w=[f"X{c}"], sem=f"lp{c % 2}", accum=True)

            def rhsH(k, t0, tn):
                return H[:, k, t0:t0 + tn], "H"

            for l in range(DEPTH):
                vb = l * V_PER
                W_in = w_in[l]
                with ExitStack() as ms:
                    YC = sb(ms, "YC", [128, 16, T], BF16)
                    bg_need(l, 32)
                    norm_to_H(A1[l], lambda c: MOD[l][:, c, vi:vi + 1], T, vi, [f"A1_{l}", f"MOD{l}_0"])

                    big = T > 512
                    with (ExitStack() if big else nullcontext(ms)) as hs:
                        SG = sb(hs, "SG", [128, T], BF16)
                        V32 = sb(hs, "V32", [CH, NCH, 128], BF16)
                        VF = sb(hs, "VF", [128, T], BF16)
                        Q = sb(hs, "Q", [128, T], BF16)
                        O = sb(hs, "O", [128, T])
                        CE = sb(hs, "CE", [128, T + 1])
                        KK = sb(hs, "KK", [128, T], BF16)
                        DD = sb(hs, "DD", [128, T])
                        EB = sb(hs, "EB", [128, T], BF16)
                        QF = sb(hs, "QF", [128, T], BF16)
                        QR = sb(hs, "QR", [128, T], BF16)
                        KA = sb(hs, "KA", [128, T], BF16)
                        KB = sb(hs, "KB", [128, T], BF16)
                        DEC = sb(hs, "DEC", [128, NCH])
                        AL = sb(hs, "AL", [128, NCH])
                        BE = sb(hs, "BE", [128, NCH])
                        SR = sb(hs, "SR", [128, 128])
                        SAL = [sb(hs, f"SAL{i}", [128, 128], BF16) for i in range(2)]
                        KXT = [sb(hs, f"KXT{i}", [CH, 128], BF16) for i in range(2)]
                        UT = sb(hs, "UT", [128, 128])
                        ATB = [sb(hs, f"ATB{i}", [CH, CH], BF16) for i in range(4)]
                        for hh in range(0 if 'nohgrn' in flags else 4):
                            base = hh * 640

                            def v3(t_, lo=0, hi=None):
                                v = t_[:, 0:T].rearrange("p (n c) -> p n c", c=CH)
                                return v if hi is None else v[:, :, lo:hi]

                            def dir_process(d):
                                V(lambda e: e.memset(CE[:, 0:1], 0.0), [], ["CE"])
                                V(lambda e: e.tensor_tensor_scan(out=CE[:, 1:T + 1], data0=ONEF[:, 0:T],
                                                                 data1=CE[:, 1:T + 1], initial=0.0,
                                                                 op0=OP.mult, op1=OP.add), ["CE", "ONEF"], ["CE"])
                                L0 = CE[:, 0:T:CH]
                                LM = CE[:, HF:T:CH]
                                L1 = CE[:, CH:T + 1:CH]
                                tt(DEC[:, :], L1, L0, OP.subtract, ["CE"], ["DEC"])
                                act(DEC[:, :], DEC[:, :], AF.Exp, ["DEC"], ["DEC"])
                                lo_, hi_ = (AL, BE) if d == 0 else (BE, AL)
                                tt(lo_[:, :], LM, L0, OP.subtract, ["CE"], ["AB"])
                                tt(hi_[:, :], L1, LM, OP.subtract, ["CE"], ["AB"])
                                act(AL[:, :], AL[:, :], AF.Exp, ["AB"], ["AB"])
                                act(BE[:, :], BE[:, :], AF.Exp, ["AB"], ["AB"])
                                Cv = CE[:, 1:T + 1] if d == 0 else CE[:, 0:T]
                                tt(v3(DD), Cv.rearrange("p (n c) -> p n c", c=CH),
                                   LM.unsqueeze(2).to_broadcast([128, NCH, CH]), OP.subtract, ["CE"], ["DD"])
                                sQ, sK = (1.0, -1.0) if d == 0 else (-1.0, 1.0)
                                act(EB[:, 0:T], DD[:, 0:T], AF.Exp, ["DD"], ["EB"], scale=sQ)
                                tt(QF[:, 0:T], Q[:, 0:T], EB[:, 0:T], OP.mult, ["Q", "EB"], ["QF"])
                                rl, rh = (HF, CH) if d == 0 else (0, HF)
                                al, ah = (0, HF) if d == 0 else (HF, CH)
                                bl, bh = (HF, CH) if d == 0 else (0, HF)
                                S.op("pool", lambda e: e.memset(QR[:, 0:T], 0.0), [], ["QR"])
                                S.op("pool", lambda e: e.memset(KA[:, 0:T], 0.0), [], ["KA"])
                                S.op("pool", lambda e: e.memset(KB[:, 0:T], 0.0), [], ["KB"])
                                V(lambda e: e.tensor_copy(out=v3(QR, rl, rh), in_=v3(QF, rl, rh)), ["QF", "QR"], ["QR"])
                                act(EB[:, 0:T], DD[:, 0:T], AF.Exp, ["DD"], ["EB"], scale=sK)
                                tt(v3(KA, al, ah), v3(KK, al, ah), v3(EB, al, ah), OP.mult, ["KK", "EB", "KA"], ["KA"])
                                tt(v3(KB, bl, bh), v3(KK, bl, bh), v3(EB, bl, bh), OP.mult, ["KK", "EB", "KB"], ["KB"])
                                tt(v3(EB), v3(EB), BE[:, :].unsqueeze(2).to_broadcast([128, NCH, CH]), OP.mult, ["EB", "AB"], ["EB"])
                                tt(EB[:, 0:T], KK[:, 0:T], EB[:, 0:T], OP.mult, ["KK", "EB"], ["EB"])
                                MASK = MASKF if d == 0 else MASKB
                                order, seg_first, seg_last = [], {}, {}
                                for (s0_, sl_, px_) in (segs if d == 0 else segs[::-1]):
                                    cs_ = list(range(s0_ // CH, (s0_ + sl_) // CH))
                                    if d == 1:
                                        cs_ = cs_[::-1]
                                    seg_first[len(order)] = px_
                                    order.extend(cs_)
                                    seg_last[len(order) - 1] = px_
                                TBK = [(PT, "PT"), (PS_, "PS")]
                                SBK = [(PA[0], "PA0"), (PA[1], "PA1")]
                                UBK = [(PA[2], "PA2"), (PA[3], "PA3")]

                                def stage_ab(it):
                                    n = order[it]
                                    a = n * CH
                                    s2 = it % 2
                                    tb, tk = TBK[s2]
                                    mm(tb[0:CH, 0:128], EB[:, a:a + CH], IDB[:, :], True, True, ["EB", "IDB"], [tk], True)
                                    act(KXT[s2][:, :], tb[0:CH, 0:128], AF.Copy, [tk], [f"KXT{s2}"])
                                    sbk, sk = SBK[s2]
                                    mm(sbk[0:CH, 0:CH], KA[:, a:a + CH], QF[:, a:a + CH], True, False, ["KA", "QF"], [sk], False)
                                    mm(sbk[0:CH, 0:CH], KB[:, a:a + CH], QR[:, a:a + CH], False, True, ["KB", "QR"], [sk], True)
                                    tt(ATB[it % 4][:, :], sbk[0:CH, 0:CH], MASK, OP.mult, [sk, "CST"], [f"ATB{it % 4}"])
                                stage_ab(0)
                                for it, n in enumerate(order):
                                    a = n * CH
                                    s2 = it % 2
                                    s4 = it % 4
                                    if it + 1 < NCH:
                                        stage_ab(it + 1)
                                    if it in seg_first:
                                        if vi == 1:
                                            S.dma("sp", SR[:, :], st0[l, d, hh, :, :], w=["SR"], sem="ls0")
                                        else:
                                            V(lambda e: e.memset(SR[:, :], 0.0), [], ["SR"])
                                    ub, uk = UBK[s2]
                                    mm(ub[:, 0:128], KXT[s2][:, :], V32[:, n, :], True, True, [f"KXT{s2}", "V32"], [uk], True)
                                    ts(SAL[s2][:, :], SR[:, :], AL[:, n:n + 1], None, OP.mult, OP.bypass, ["SR", "AB"], [f"SAL{s2}"])
                                    stt(SR[:, :], SR[:, :], DEC[:, n:n + 1], ub[:, 0:128], OP.mult, OP.add, ["SR", "DEC", uk], ["SR"])
                                    pcol = (n % NPB) * CH
                                    mm(PO[:, pcol:pcol + CH], V32[:, n, :], ATB[s4][:, :], True, False, ["V32", f"ATB{s4}"], ["PO"], False)
                                    mm(PO[:, pcol:pcol + CH], SAL[s2][:, :], QF[:, a:a + CH], False, True, [f"SAL{s2}", "QF"], ["PO"], True)
                                    blk_done = (n % NPB == NPB - 1 or n == NCH - 1) if d == 0 else (n % NPB == 0)
                                    if blk_done:
                                        b0 = (n // NPB) * NPB * CH
                                        b1 = min(T, b0 + NPB * CH)
                                        if d == 0:
                                            act(O[:, b0:b1], PO[:, 0:b1 - b0], AF.Copy, ["PO"], ["O"])
                                        else:
                                            tt(O[:, b0:b1], PO[:, 0:b1 - b0], O[:, b0:b1], OP.add, ["PO", "O"], ["O"])
                                    if seg_last.get(it, -1) >= 0:
                                        sname = f"so{d}"
                                        S.dma("sp", nst[seg_last[it], l, d, hh, :, :], SR[:, :], r=["SR"], sem=sname)
                                        if sname not in out_sems:
                                            out_sems.append(sname)

                            def ev_qg(f, t0, tn, pa, pkey):
                                if f == 0:
                                    act(Q[:, t0:t0 + tn], pa, AF.Copy, [pkey], ["Q"])
                                else:
                                    act(SG[:, t0:t0 + tn], pa, AF.Silu, [pkey], ["SG"])

                            def ev_gate(f, t0, tn, pa, pkey):
                                col = l * 8 + f * 4 + hh
                                tm, tkey = next_tm()
                                act(tm[:, 0:tn], pa, AF.Sigmoid, [pkey], [tkey])
                                ts(tm[:, 0:tn], tm[:, 0:tn], OML[:, col:col + 1], LB[:, col:col + 1], OP.mult, OP.add,
                                   [tkey, "LB", "OML"], [tkey])
                                act(CE[:, 1 + t0:1 + t0 + tn], tm[:, 0:tn], AF.Ln, [tkey], ["CE"])
                                act(KK[:, t0:t0 + tn], tm[:, 0:tn], AF.Identity, [tkey], ["KK"],
                                    bias=ONEC[:, 0:1], scale=-1.0)
                                if t0 + tn == T:
                                    dir_process(f)
                            proj_fm(W_in, 0, 16, base + 256, 256, rhsH, T, ev_qg)
                            def ev_v(f, t0, tn, pa, pkey):
                                act(VF[:, t0:t0 + tn], pa, AF.Copy, [pkey], ["VF"])
                                if t0 + tn < T:
                                    return
                                for g4 in range(NCH // 4):
                                    bk, bkey = next_pa()
                                    for j in range(4):
                                        a = (g4 * 4 + j) * CH
                                        mm(bk[0:CH, j * 128:(j + 1) * 128], VF[:, a:a + CH], IDB[:, :], True, True,
                                           ["VF", "IDB"], [bkey], j == 3)
                                    act(V32[:, g4 * 4:g4 * 4 + 4, :], bk[0:CH, 0:512].rearrange("p (j f) -> p j f", f=128),
                                        AF.Copy, [bkey], ["V32"])
                            proj_fm(W_in, 0, 16, base + 512, 128, rhsH, T, ev_v)
                            proj_fm(W_in, 0, 16, base, 256, rhsH, T, ev_gate)
                            for (t0, tn) in TB:
                                sumsq_bcast(lambda c: (O[:, t0:t0 + tn], "O"), 1, t0, tn)
                                rstd_from(PS_[:, 0:tn], "PS", tn, 128)
                                tm, tkey = next_tm()
                                stt(tm[:, 0:tn], O[:, t0:t0 + tn], VEC[:, vb + V_HN + hh:vb + V_HN + hh + 1], RS[:, 0:tn],
                                    OP.mult, OP.mult, ["O", "VEC", "RS"], [tkey])
                                tt(YC[:, hh, t0:t0 + tn], tm[:, 0:tn], SG[:, t0:t0 + tn], OP.mult, [tkey, "SG"], ["YC"])
                        if big:
                            S.barrier()

                    with (ExitStack() if big else nullcontext(ms)) as bs:
                        NT = T // 128
                        GU = sb(bs, "GU", [128, 4, T], BF16)
                        VN = sb(bs, "VN", [128, NT, 512], BF16)
                        XS = sb(bs, "XS", [128, 512])
                        X2 = sb(bs, "X2", [128, 512])
                        SS1 = sb(bs, "SS1", [128, 1])
                        SGB = sb(bs, "SGB", [128, 512])
                        BSB = sb(bs, "BSB", [128, 512])
                        WST = sb(bs, "WST", [128, 4, 128], BF16)
                        S.dma("sp", SGB[:, :], sgn[l].partition_broadcast(128), w=["SGB"], sem="l0")
                        S.dma("sp", BSB[:, :], bsg[l].partition_broadcast(128), w=["BSB"], sem="l1")
                        S.dma("pool", WST[:, :, :], wsgT[l].rearrange("h s p -> s h p"), w=["WST"], sem="l3")

                        def gelu_to(dst, dkey, pa, pkey, n):
                            act(dst, pa, AF.Gelu_apprx_tanh, [pkey], [dkey])

                        def ev_u(off):
                            def ev(f, t0, tn, pa, pkey):
                                gelu_to(GU[:, off + f, t0:t0 + tn], "GU", pa, pkey, tn)
                            return ev
                        cb = 2560
                        proj_fm(W_in, 0, 16, cb, 256, rhsH, T, ev_u(0))
                        proj_fm(W_in, 0, 16, cb + 256, 256, rhsH, T, ev_u(2))
                        wv0, wk0 = load_w(W_in, 0, 16, cb + 512, 256)
                        wv1, wk1 = load_w(W_in, 0, 16, cb + 768, 256)
                        for n in range(NT):
                            pa, pkey = next_pa()
                            for hf, (wv, wk) in enumerate(((wv0, wk0), (wv1, wk1))):
                                for j in range(2):
                                    cj = hf * 2 + j
                                    for k in range(16):
                                        mm(pa[:, cj * 128:(cj + 1) * 128], H[:, k, n * 128:(n + 1) * 128], wv[:, j, k, :],
                                           k == 0, k == 15, [wk, "H"], [pkey], k == 15)
                            tm, tkey = next_tm()
                            gelu_to(tm[:, :], tkey, pa[:, :], pkey, 512)
                            act(X2[:, :], tm[:, :], AF.Square, [tkey], ["X2"])
                            V(lambda e: e.tensor_reduce(out=SS1[:, 0:1], in_=X2[:, :], axis=mybir.AxisListType.X, op=OP.add),
                              ["X2"], ["SS1"])
                            act(SS1[:, :], SS1[:, :], AF.Ln, ["SS1"], ["SS1"], bias=EPSC[:, 0:1], scale=1.0 / 512)
                            act(SS1[:, :], SS1[:, :], AF.Exp, ["SS1"], ["SS1"], scale=-0.5)
                            stt(VN[:, n, :], tm[:, :], SS1[:, 0:1], SGB[:, :], OP.mult, OP.mult, [tkey, "SS1", "SGB"], ["VN"])
                        for hb in range(4):
                            for n in range(NT):
                                mm(PT[:, 0:128], VN[:, n, hb * 128:(hb + 1) * 128], WST[:, hb, :], True, True,
                                   ["VN", "WST"], ["PTsgu"], True)
                                tm, tkey = next_tm()
                                tt(tm[:, 0:128], PT[:, 0:128], BSB[:, hb * 128:(hb + 1) * 128], OP.add, ["PTsgu", "BSB"], [tkey])
                                tt(YC[:, 4 + hb, n * 128:(n + 1) * 128], tm[:, 0:128], GU[:, hb, n * 128:(n + 1) * 128], OP.mult,
                                   [tkey, "GU"], ["YC"])
                        if big:
                            S.barrier()

                    with ExitStack() as cs:
                        CC = sb(cs, "CC", [128, 4, T])
                        CY = sb(cs, "CY", [128, 4, T])
                        cb = 3584

                        def ev_cc(off):
                            def ev(f, t0, tn, pa, pkey):
                                act(CC[:, off + f, t0:t0 + tn], pa, AF.Copy, [pkey], [f"CC{off + f}"])
                            return ev

                        def ev_ch(off):
                            def ev(f, t0, tn, pa, pkey):
                                j = off + f
                                tt(CC[:, j, t0:t0 + tn], pa, CC[:, j, t0:t0 + tn], OP.mult, [pkey, f"CC{j}"], [f"CC{j}"])
                                if t0 + tn == T:
                                    w0 = VEC[:, vb + V_WC + 0 * 4 + j:vb + V_WC + 0 * 4 + j + 1]
                                    w1 = VEC[:, vb + V_WC + 1 * 4 + j:vb + V_WC + 1 * 4 + j + 1]
                                    w2 = VEC[:, vb + V_WC + 2 * 4 + j:vb + V_WC + 2 * 4 + j + 1]
                                    act(CY[:, j, 0:T], CC[:, j, 0:T], AF.Copy, [f"CC{j}"], [f"CY{j}"], scale=w1)
                                    for (s0_, sl_, _px) in segs:
                                        e0_ = s0_ + sl_
                                        stt(CY[:, j, s0_ + 1:e0_], CC[:, j, s0_:e0_ - 1], w0, CY[:, j, s0_ + 1:e0_], OP.mult, OP.add, [f"CC{j}", f"CY{j}"], [f"CY{j}"])
                                        stt(CY[:, j, s0_:e0_ - 1], CC[:, j, s0_ + 1:e0_], w2, CY[:, j, s0_:e0_ - 1], OP.mult, OP.add, [f"CC{j}", f"CY{j}"], [f"CY{j}"])
                            return ev

                        def ev_cb(off):
                            def ev(f, t0, tn, pa, pkey):
                                j = off + f
                                tt(YC[:, 8 + j, t0:t0 + tn], pa, CY[:, j, t0:t0 + tn], OP.mult, [pkey, f"CY{j}"], ["YC"])
                            return ev
                        proj_fm(W_in, 0, 16, cb + 512, 256, rhsH, T, ev_cc(0))
                        proj_fm(W_in, 0, 16, cb + 768, 256, rhsH, T, ev_cc(2))
                        proj_fm(W_in, 0, 16, cb + 1024, 256, rhsH, T, ev_ch(0))
                        proj_fm(W_in, 0, 16, cb + 1280, 256, rhsH, T, ev_ch(2))
                        proj_fm(W_in, 0, 16, cb, 256, rhsH, T, ev_cb(0))
                        proj_fm(W_in, 0, 16, cb + 256, 256, rhsH, T, ev_cb(2))
                        S.barrier()

                    with ExitStack() as dsx:
                        DX = sb(dsx, "DX", [128, T])
                        CS = sb(dsx, "CS", [128, T + 1])
                        SM = sb(dsx, "SM", [128, T])
                        DF = sb(dsx, "DF", [128, T], BF16)
                        WP = sb(dsx, "WP", [128, 4, 128], BF16)
                        S.dma("pool", WP[:, :, :], wpool[l].rearrange("g i o -> i g o"), w=["WP"], sem="l3")
                        cb = 5120

                        def ev_d(off):
                            def ev(f, t0, tn, pa, pkey):
                                gi = off + f
                                win = POOLW[gi]
                                hw = win // 2
                                act(DX[:, t0:t0 + tn], pa, AF.Copy, [pkey], ["DX"])
                                if t0 + tn < T:
                                    return
                                for (s0_, L_, _px) in segs:
                                    V(lambda e: e.memset(CS[:, 0:1], 0.0), [], ["CS"])
                                    V(lambda e: e.tensor_tensor_scan(out=CS[:, 1:L_ + 1], data0=ONEF[:, 0:L_], data1=DX[:, s0_:s0_ + L_],
                                                                     initial=0.0, op0=OP.mult, op1=OP.add), ["DX", "ONEF"], ["CS"])
                                    SMs = SM[:, s0_:s0_ + L_]
                                    tt(SMs[:, hw:L_ - hw + 1], CS[:, 2 * hw:L_ + 1], CS[:, 0:L_ - 2 * hw + 1], OP.subtract, ["CS"], ["SM"])
                                    V(lambda e: e.tensor_copy(out=SMs[:, 0:hw], in_=CS[:, hw:2 * hw]), ["CS"], ["SM"])
                                    if hw > 1:
                                        ts(SMs[:, L_ - hw + 1:L_], CS[:, L_ - 2 * hw + 1:L_ - hw], -1.0, CS[:, L_:L_ + 1], OP.mult, OP.add, ["CS"], ["SM"])
                                    ts(SMs[:, hw:L_ - hw + 1], SMs[:, hw:L_ - hw + 1], 1.0 / win, None, OP.mult, OP.bypass, ["SM"], ["SM"])
                                    for t in range(hw):
                                        ts(SMs[:, t:t + 1], SMs[:, t:t + 1], 1.0 / (t + hw), None, OP.mult, OP.bypass, ["SM"], ["SM"])
                                    for t in range(L_ - hw + 1, L_):
                                        ts(SMs[:, t:t + 1], SMs[:, t:t + 1], 1.0 / (L_ - t + hw), None, OP.mult, OP.bypass, ["SM"], ["SM"])
                                tt(DF[:, 0:T], SM[:, 0:T], DX[:, 0:T], OP.subtract, ["SM", "DX"], ["DF"])
                                for (u0, un) in TB:
                                    pa2, pk2 = next_pa()
                                    mm(pa2[:, 0:un], WP[:, gi, :], DF[:, u0:u0 + un], True, True, ["WP", "DF"], [pk2], True)
                                    act(YC[:, 12 + gi, u0:u0 + un], pa2[:, 0:un], AF.Copy, [pk2, "VEC"], ["YC"],
                                        scale=VEC[:, vb + V_PS + gi:vb + V_PS + gi + 1])
                            return ev
                        proj_fm(W_in, 0, 16, cb, 256, rhsH, T, ev_d(0))
                        proj_fm(W_in, 0, 16, cb + 256, 256, rhsH, T, ev_d(2))
                        if big:
                            S.barrier()

                    def rhsY(k, t0, tn):
                        return YC[:, k, t0:t0 + tn], "YC"

                    def ev_out(db):
                        def ev(f, t0, tn, pa, pkey):
                            dch = db * 2 + f
                            stt(X[:, dch, t0:t0 + tn], pa, MOD[l][:, 32 + dch, vi:vi + 1], X[:, dch, t0:t0 + tn],
                                OP.mult, OP.add, [pkey, f"MOD{l}_2", f"X{dch}"], [f"X{dch}"])
                        return ev
                    bg_need(l, 48)
                    for db in range(8):
                        proj_fm(w_out[l], 0, 16, db * 256, 256, rhsY, T, ev_out(db))
                    S.barrier()

                bg_need(l, 80)
                norm_to_H(A2[l], lambda c: MOD[l][:, 48 + c, vi:vi + 1], T, vi, [f"A2_{l}", f"MOD{l}_3"])
                qoff = 0
                with ExitStack() as fs:
                  G = sb(fs, "G", [128, 12, T], BF16)
                  CVV = sb(fs, "CVV", [128, 2, T])
                  CVG = [sb(fs, f"CVG{i}", [128, T]) for i in range(2)]
                  cvg_i = [0]
                  for nq in (() if 'noffn' in flags else (12, 12, 10, 10)):
                    if True:

                        def ev_up(is_gate, i0):
                            pend = []

                            def ev(f, t0, tn, pa, pkey):
                                i = i0 + f
                                ch = (44 if is_gate else 0) + qoff + i
                                w0 = VEC[:, vb + V_WCF + 0 * 88 + ch:vb + V_WCF + 0 * 88 + ch + 1]
                                w1 = VEC[:, vb + V_WCF + 1 * 88 + ch:vb + V_WCF + 1 * 88 + ch + 1]
                                w2 = VEC[:, vb + V_WCF + 2 * 88 + ch:vb + V_WCF + 2 * 88 + ch + 1]
                                if is_gate:
                                    if t0 == 0:
                                        cvg_i[0] += 1
                                    gi_ = cvg_i[0] % 2
                                    cvt, ck = CVG[gi_][:, :], f"CVG{gi_}"
                                else:
                                    cvt, ck = CVV[:, f, :], f"CVV{f}"
                                act(cvt[:, t0:t0 + tn], pa, AF.Copy, [pkey, "VEC"], [ck], scale=w1)
                                pend.append((t0, tn, pa, pkey))
                                if t0 + tn < T:
                                    return
                                for (s0_, sl_, _px) in segs:
                                    e0_ = s0_ + sl_
                                    for bi, (b0, bn, pb, pbk) in enumerate(pend):
                                        lo, hi = max(s0_, b0), min(e0_, b0 + bn)
                                        if hi - lo < 2:
                                            continue
                                        stt(cvt[:, lo + 1:hi], pb[:, lo - b0:hi - 1 - b0], w0, cvt[:, lo + 1:hi], OP.mult, OP.add, [pbk, ck], [ck])
                                        stt(cvt[:, lo:hi - 1], pb[:, lo + 1 - b0:hi - b0], w2, cvt[:, lo:hi - 1], OP.mult, OP.add, [pbk, ck], [ck])
                                        if bi + 1 < len(pend) and hi < e0_ and hi == b0 + bn:
                                            nb0, nbn, npb, npbk = pend[bi + 1]
                                            stt(cvt[:, hi:hi + 1], pb[:, bn - 1:bn], w0, cvt[:, hi:hi + 1], OP.mult, OP.add, [pbk, ck], [ck])
                                            stt(cvt[:, hi - 1:hi], npb[:, 0:1], w2, cvt[:, hi - 1:hi], OP.mult, OP.add, [npbk, ck], [ck])
                                pend.clear()
                                if is_gate:
                                    act(cvt[:, 0:T], cvt[:, 0:T], AF.Silu, [ck], [ck])
                                    tt(G[:, i, 0:T], CVV[:, f, 0:T], cvt[:, 0:T], OP.mult, [f"CVV{f}", ck], ["G"])
                            return ev
                        for i0 in range(0, nq, 2):
                            proj_fm(w_up[l], 0, 16, (qoff + i0) * 128, 256, rhsH, T, ev_up(False, i0))
                            proj_fm(w_up[l], 0, 16, DFF + (qoff + i0) * 128, 256, rhsH, T, ev_up(True, i0))

                        def rhsG(k, t0, tn):
                            return G[:, k, t0:t0 + tn], "G"

                        def ev_dn(db):
                            def ev(f, t0, tn, pa, pkey):
                                dch = db * 2 + f
                                stt(X[:, dch, t0:t0 + tn], pa, MOD[l][:, 80 + dch, vi:vi + 1], X[:, dch, t0:t0 + tn],
                                    OP.mult, OP.add, [pkey, f"MOD{l}_5", f"X{dch}"], [f"X{dch}"])
                            return ev
                        bg_need(l, 96)
                        for db in range(8):
                            proj_fm(w_down[l], qoff, nq, db * 256, 256, rhsG, T, ev_dn(db))
                    qoff += nq
                  S.barrier()

            yTv = yT.rearrange("(c p) t -> p c t", p=128)
            for (t0, tn) in TB:
                sumsq_bcast(lambda c: (X[:, c, t0:t0 + tn], f"X{c}"), 16, t0, tn)
                rstd_from(PS_[:, 0:tn], "PS", tn, D)
                for c in range(16):
                    tm, tkey = next_tm()
                    stt(tm[:, 0:tn], X[:, c, t0:t0 + tn], VEC[:, V_FIN + c:V_FIN + c + 1], RS[:, 0:tn], OP.mult, OP.mult,
                        [f"X{c}", "VEC", "RS"], [tkey])
                    sname = f"sy{c % 2}"
                    S.dma("sp", yTv[:, c, tok0 + t0:tok0 + t0 + tn], tm[:, 0:tn], r=[tkey], sem=sname)
                    if sname not in out_sems:
                        out_sems.append(sname)
            S.barrier()
            gscope.close()

        need = {s_: S.dsem[s_][1] for s_ in out_sems}
        S._wait("sp", need)
    return nc


def _fm(v):
    v = np.asarray(v, np.float32)
    return np.ascontiguousarray(v.reshape(-1, 128).T)


def _pos_embed_T():
    rows = TS // 64
    r = np.repeat(np.arange(rows), 64).astype(np.float32)[:, None]
    col = np.tile(np.arange(64), rows).astype(np.float32)[:, None]
    quarter = D // 4
    freq = np.exp(np.float32(-np.log(10000.0)) * np.arange(quarter, dtype=np.float32) / np.float32(quarter)).astype(np.float32)[None, :]
    emb = np.concatenate([np.sin(r * freq), np.cos(r * freq), np.sin(col * freq), np.cos(col * freq)], -1).astype(np.float32)
    return np.ascontiguousarray(emb.T)


_PROG = {}


def kernel(x_prompt, x_sample, c, state_hgrn, c_ctx, w_ada, b_ada, norm_mix, norm_ffn, w_in,
           lb_logits, hgrn_norm, sgu_norm, w_sgu, b_sgu, w_conv_c, w_pool, pool_scale, w_out,
           w_up, w_conv_ffn, w_down, norm_final):
    f = lambda a: np.asarray(a, np.float32)
    x_prompt, x_sample, c, state_hgrn, c_ctx = f(x_prompt), f(x_sample), f(c), f(state_hgrn), f(c_ctx)
    w_ada, w_in, w_out, w_up, w_down = f(w_ada), f(w_in), f(w_out), f(w_up), f(w_down)
    vecs = np.zeros((128, NV), np.float32)
    for l in range(DEPTH):
        vb = l * V_PER
        vecs[:, vb + V_BADA:vb + V_BADA + 96] = _fm(b_ada[l])
        vecs[:, vb + V_NM:vb + V_NM + 16] = _fm(norm_mix[l])
        vecs[:, vb + V_NF:vb + V_NF + 16] = _fm(norm_ffn[l])
        vecs[:, vb + V_HN:vb + V_HN + 4] = _fm(hgrn_norm[l])
        wc = f(w_conv_c)[l]
        for tap in range(3):
            vecs[:, vb + V_WC + tap * 4:vb + V_WC + tap * 4 + 4] = _fm(wc[tap])
        vecs[:, vb + V_PS:vb + V_PS + 4] = _fm(pool_scale[l])
        wcf = f(w_conv_ffn)[l]
        for tap in range(3):
            vecs[:, vb + V_WCF + tap * 88:vb + V_WCF + tap * 88 + 88] = _fm(wcf[tap])
        lbl = f(lb_logits)[l]
        for d in range(2):
            vecs[:, vb + V_LBL + d * 4:vb + V_LBL + d * 4 + 4] = _fm(lbl[d])
    vecs[:, V_FIN:V_FIN + 16] = _fm(norm_final)
    consts = np.zeros((128, 384), np.float32)
    consts[:, 0:128] = np.eye(128, dtype=np.float32)
    ii = np.arange(64)
    consts[0:64, 128:192] = (ii[None, :] >= ii[:, None]).astype(np.float32)
    consts[0:64, 256:320] = (ii[:, None] >= ii[None, :]).astype(np.float32)
    perm = []
    for h in range(4):
        for grp in (2, 3, 0, 4, 1):
            perm.extend(range(grp * 512 + h * 128, grp * 512 + (h + 1) * 128))
    perm.extend(range(2560, 5632))
    def tile_w(w):
        L, K, N = w.shape
        return np.ascontiguousarray(w.reshape(L, K // 128, 128, N // 128, 128).transpose(0, 3, 2, 1, 4)).reshape(L, N // 128, 128, K)
    w_in_p = tile_w(w_in[:, :, np.asarray(perm)])
    w_ada, w_out, w_up, w_down = tile_w(w_ada), tile_w(w_out), tile_w(w_up), tile_w(w_down)
    wsgT = np.ascontiguousarray(np.transpose(f(w_sgu), (0, 1, 3, 2)))
    posT = _pos_embed_T()
    sgn = np.ascontiguousarray(f(sgu_norm))
    bsg = np.ascontiguousarray(f(b_sgu).reshape(DEPTH, 512))
    wpl = np.ascontiguousarray(f(w_pool))

    in_maps = []
    for core in range(NCORES):
        b = core % 2
        xT = np.ascontiguousarray(np.concatenate(
            [x_prompt[2 * core].T, x_prompt[2 * core + 1].T, x_sample[b].T], axis=1))
        cvv = np.stack([_fm(c_ctx), _fm(c[b])], axis=-1)
        in_maps.append(dict(xT=xT, posT=posT, cv=np.ascontiguousarray(cvv), st0=np.ascontiguousarray(state_hgrn[b]),
                            vecs=vecs, sgn=sgn, bsg=bsg, wsgT=wsgT, wpool=wpl, consts=consts,
                            w_ada=w_ada, w_in=w_in_p, w_out=w_out, w_up=w_up, w_down=w_down))
    if "nc" not in _PROG:
        _PROG["nc"] = build_program()
    res = run_bass_kernel_spmd(_PROG["nc"], in_maps, core_ids=list(range(NCORES)))
    y_prompt = np.zeros((16, TP, D), np.float32)
    y_sample = np.zeros((2, TS, D), np.float32)
    new_state = np.zeros((16, DEPTH, 2, 4, 128, 128), np.float32)
    for core in range(NCORES):
        r = res.results[core]
        yT = r["yT"]
        y_prompt[2 * core] = yT[:, 0:TP].T
        y_prompt[2 * core + 1] = yT[:, TP:2 * TP].T
        if core < 2:
            y_sample[core] = yT[:, 2 * TP:].T
        new_state[2 * core] = r["nst"][0]
        new_state[2 * core + 1] = r["nst"][1]
    return (y_prompt, y_sample, new_state)
```

```python
from contextlib import ExitStack, nullcontext
import numpy as np
import concourse.bass as bass
import concourse.mybir as mybir
from concourse.bass_utils import run_bass_kernel_spmd

F32 = mybir.dt.float32
BF16 = mybir.dt.bfloat16
AF = mybir.ActivationFunctionType
OP = mybir.AluOpType

D = 2048
DEPTH = 2
DFF = 5632
NCORES = 8
TP = 256
TS = 1024
TTOT = 2 * TP + TS
EPS = 1e-6
POOLW = (2, 4, 8, 16)
V_BADA, V_NM, V_NF, V_HN, V_WC, V_PS, V_WCF, V_LBL = 0, 96, 112, 128, 132, 144, 148, 412
V_PER = 420
V_FIN = 2 * V_PER
NV = V_FIN + 16


class Sch:
    def __init__(self, nc, st):
        self.nc = nc
        self.st = st
        self.E = {}
        for n, eng in [("pe", nc.tensor), ("dve", nc.vector), ("act", nc.scalar),
                       ("pool", nc.gpsimd), ("sp", nc.sync)]:
            sem = st.enter_context(nc.semaphore("s_" + n))
            self.E[n] = dict(eng=eng, sem=sem, cnt=0, waited={})
        self.dsem = {}
        self.lw = {}
        self.rd = {}

    def semof(self, src):
        if src in self.E:
            return self.E[src]["sem"]
        return self.dsem[src][0]

    def _wait(self, en, need):
        e = self.E[en]
        for src, val in need.items():
            if src == en and en == "pe":
                continue
            if e["waited"].get(src, 0) < val:
                e["eng"].wait_ge(self.semof(src), val)
                e["waited"][src] = val

    def _deps(self, r, w):
        need = {}

        def add(t):
            if t is not None:
                need[t[0]] = max(need.get(t[0], 0), t[1])
        for k in r:
            add(self.lw.get(k))
        for k in w:
            add(self.lw.get(k))
            for s_, v_ in self.rd.get(k, {}).items():
                add((s_, v_))
        return need

    def _reg(self, src, tval, r, w):
        for k in r:
            d = self.rd.setdefault(k, {})
            d[src] = max(d.get(src, 0), tval)
        for k in w:
            self.lw[k] = (src, tval)
            self.rd[k] = {}

    def op(self, en, fn, r=(), w=(), inc=True):
        e = self.E[en]
        self._wait(en, self._deps(r, w))
        ins = fn(e["eng"])
        if inc:
            e["cnt"] += 1
            ins.then_inc(e["sem"], 1)
            tval = e["cnt"]
        else:
            tval = e["cnt"] + 1
        self._reg(en, tval, r, w)

    def dma(self, q, out, in_, r=(), w=(), sem="d0", accum=False):
        if sem not in self.dsem:
            self.dsem[sem] = [self.st.enter_context(self.nc.semaphore("dq_" + sem)), 0]
        ds = self.dsem[sem]
        need = self._deps(r, w)
        if ds[1] > 0:
            need[sem] = max(need.get(sem, 0), ds[1])
        self._wait(q, need)
        if accum:
            self.E[q]["eng"].dma_start(out=out, in_=in_, accum_op=OP.add).then_inc(ds[0], 16)
        else:
            self.E[q]["eng"].dma_start(out=out, in_=in_).then_inc(ds[0], 16)
        ds[1] += 16
        self._reg(sem, ds[1], r, w)

    def barrier(self):
        for en in self.E:
            need = {o: self.E[o]["cnt"] for o in self.E if self.E[o]["cnt"] > 0}
            for s_, d_ in self.dsem.items():
                if d_[1] > 0:
                    need[s_] = d_[1]
            self._wait(en, need)
        self.lw = {}
        self.rd = {}


def tblocks(T):
    return [(t0, min(512, T - t0)) for t0 in range(0, T, 512)]


def build_program(flags=()):
    nc = bass.Bass("TRN2", target_bir_lowering=False)

    def din(name, shape):
        return nc.dram_tensor(name, list(shape), F32, kind="ExternalInput").ap()
    xT = din("xT", [D, TTOT])
    posT = din("posT", [D, TS])
    cv = din("cv", [128, 16, 2])
    st0 = din("st0", [DEPTH, 2, 4, 128, 128])
    vecs = din("vecs", [128, NV])
    sgn = din("sgn", [DEPTH, 512])
    bsg = din("bsg", [DEPTH, 512])
    wsgT = din("wsgT", [DEPTH, 4, 128, 128])
    wpool = din("wpool", [DEPTH, 4, 128, 128])
    consts = din("consts", [128, 384])
    w_ada = din("w_ada", [DEPTH, 96, 128, 16 * 128])
    w_in = din("w_in", [DEPTH, 44, 128, 16 * 128])
    w_out = din("w_out", [DEPTH, 16, 128, 16 * 128])
    w_up = din("w_up", [DEPTH, 88, 128, 16 * 128])
    w_down = din("w_down", [DEPTH, 16, 128, 44 * 128])
    yT = nc.dram_tensor("yT", [D, TTOT], F32, kind="ExternalOutput").ap()
    nst = nc.dram_tensor("nst", [2, DEPTH, 2, 4, 128, 128], F32, kind="ExternalOutput").ap()

    uid = [0]

    with ExitStack() as st:
        S = Sch(nc, st)

        def sb(stack, name, shape, dt=F32):
            uid[0] += 1
            return stack.enter_context(nc.sbuf_tensor(f"{name}_{uid[0]}", list(shape), dt))

        def ps(name, shape, dt=F32):
            return st.enter_context(nc.psum_tensor(name, list(shape), dt))

        X = sb(st, "X", [128, 16, TS])
        H = sb(st, "H", [128, 16, TS], BF16)
        NW = 2
        WB = [sb(st, f"WB{i}", [128, 4096], BF16) for i in range(NW)]
        VEC = sb(st, "VEC", [128, NV])
        MOD = [sb(st, f"MOD{l}", [128, 96, 2]) for l in range(DEPTH)]
        A1 = [sb(st, f"A1{l}", [128, 16, 2]) for l in range(DEPTH)]
        A2 = [sb(st, f"A2{l}", [128, 16, 2]) for l in range(DEPTH)]
        LB = sb(st, "LB", [128, 16])
        OML = sb(st, "OML", [128, 16])
        CST = sb(st, "CST", [128, 384])
        IDB = sb(st, "IDB", [128, 128], BF16)
        ONB = sb(st, "ONB", [128, 128], BF16)
        EPSC = sb(st, "EPSC", [128, 1])
        ONEC = sb(st, "ONEC", [128, 1])
        ONEF = sb(st, "ONEF", [128, TS], BF16)
        SC = sb(st, "SC", [128, 16, 2], BF16)
        CVT = sb(st, "CVT", [128, 16, 2])
        ADR = [sb(st, f"ADR{i}", [2, 256]) for i in range(2)]
        SQ = [sb(st, f"SQ{i}", [128, 512], BF16) for i in range(2)]
        RS = sb(st, "RS", [128, 512])
        RT = sb(st, "RT", [128, 512])
        TM = [sb(st, f"TM{i}", [128, 512]) for i in range(2)]
        MASKF = CST[0:32, 128:160]
        MASKB = CST[0:32, 256:288]

        PA = [ps(f"PA{i}", [128, 512]) for i in range(4)]
        PS_ = ps("PSs", [128, 512])
        PT = ps("PT", [128, 512])
        PTB = ps("PTB", [128, 512])
        PO = ps("PO", [128, 512])
        pa_i = [0]
        sq_i = [0]
        tm_i = [0]
        ws_i = [0]

        def V(fn, r, w):
            S.op("dve", fn, r, w)

        def A(fn, r, w):
            S.op("act", fn, r, w)

        def act(out, in_, func, r, w, bias=None, scale=1.0):
            if bias is None:
                S.op("act", lambda e: e.activation(out=out, in_=in_, func=func, scale=scale), r, w)
            else:
                S.op("act", lambda e: e.activation(out=out, in_=in_, func=func, bias=bias, scale=scale), r, w)

        def tt(out, in0, in1, op, r, w):
            V(lambda e: e.tensor_tensor(out=out, in0=in0, in1=in1, op=op), r, w)

        def ts(out, in0, s1, s2, op0, op1, r, w):
            V(lambda e: e.tensor_scalar(out=out, in0=in0, scalar1=s1, scalar2=s2, op0=op0, op1=op1), r, w)

        def stt(out, in0, scalar, in1, op0, op1, r, w):
            V(lambda e: e.scalar_tensor_tensor(out=out, in0=in0, scalar=scalar, in1=in1, op0=op0, op1=op1), r, w)

        def mm(out, lhsT, rhs, start, stop, r, w, inc):
            S.op("pe", lambda e: e.matmul(out, lhsT=lhsT, rhs=rhs, start=start, stop=stop), r, w, inc=inc)

        def load_w(srct, k0, nK, c0, ncols):
            slot = ws_i[0] % len(WB)
            ws_i[0] += 1
            nch = ncols // 128
            cc0 = c0 // 128
            view = WB[slot][:, 0:nch * nK * 128].rearrange("p (c k f) -> p c k f", c=nch, k=nK)
            srcv = srct[cc0:cc0 + nch, :, k0 * 128:(k0 + nK) * 128].rearrange("c p (k f) -> p c k f", k=nK)
            S.dma("pool", view, srcv, r=[], w=[f"WB{slot}"], sem=f"dw{slot}")
            return view, f"WB{slot}"

        def next_pa():
            i = pa_i[0] % 4
            pa_i[0] += 1
            return PA[i], f"PA{i}"

        def next_tm():
            i = tm_i[0] % 2
            tm_i[0] += 1
            return TM[i], f"TM{i}"

        def proj_fm(srct, k0, nK, c0, ncols, rhs_fn, T, evac):
            wv, wkey = load_w(srct, k0, nK, c0, ncols)
            for f in range(ncols // 128):
                for (t0, tn) in tblocks(T):
                    pa, pkey = next_pa()
                    for k in range(nK):
                        rhs, rkey = rhs_fn(k, t0, tn)
                        mm(pa[:, 0:tn], wv[:, f, k, :], rhs, k == 0, k == nK - 1,
                           [wkey, rkey], [pkey], k == nK - 1)
                    evac(f, t0, tn, pa[:, 0:tn], pkey)
            bg_step()

        def rstd_from(ss_ap, sskey, tn, n):
            act(RT[:, 0:tn], ss_ap, AF.Ln, [sskey], ["RT"], bias=EPSC[:, 0:1], scale=1.0 / n)
            act(RS[:, 0:tn], RT[:, 0:tn], AF.Exp, ["RT"], ["RS"], scale=-0.5)

        def sumsq_bcast(src_fn, nchunks, t0, tn):
            for c in range(nchunks):
                src, skey = src_fn(c)
                i = sq_i[0] % 2
                sq_i[0] += 1
                act(SQ[i][:, 0:tn], src, AF.Square, [skey], [f"SQ{i}"])
                mm(PS_[:, 0:tn], ONB[:, :], SQ[i][:, 0:tn], c == 0, c == nchunks - 1,
                   [f"SQ{i}", "ONB"], ["PS"], True)

        def norm_to_H(Aco, Bco_fn, T, vi, mkeys):
            for (t0, tn) in tblocks(T):
                sumsq_bcast(lambda c: (X[:, c, t0:t0 + tn], f"X{c}"), 16, t0, tn)
                rstd_from(PS_[:, 0:tn], "PS", tn, D)
                for c in range(16):
                    tm, tkey = next_tm()
                    tt(tm[:, 0:tn], X[:, c, t0:t0 + tn], RS[:, 0:tn], OP.mult, [f"X{c}", "RS"], [tkey])
                    act(H[:, c, t0:t0 + tn], tm[:, 0:tn], AF.Identity, [tkey] + mkeys, ["H"],
                        bias=Bco_fn(c), scale=Aco[:, c, vi:vi + 1])

        S.dma("sp", VEC[:, :], vecs[:, :], w=["VEC"], sem="l0")
        S.dma("sp", CST[:, :], consts[:, :], w=["CST"], sem="l1")
        S.dma("sp", CVT[:, :, :], cv[:, :, :], w=["CVT"], sem="l2")
        V(lambda e: e.tensor_copy(out=IDB[:, :], in_=CST[:, 0:128]), ["CST"], ["IDB"])
        V(lambda e: e.memset(ONB[:, :], 1.0), [], ["ONB"])
        V(lambda e: e.memset(EPSC[:, :], EPS), [], ["EPSC"])
        V(lambda e: e.memset(ONEC[:, :], 1.0), [], ["ONEC"])
        V(lambda e: e.memset(ONEF[:, :], 1.0), [], ["ONEF"])
        act(SC[:, :, :], CVT[:, :, :], AF.Silu, ["CVT"], ["SC"])
        V(lambda e: e.memset(LB[:, 0:8], 0.0), [], ["LB"])
        tt(LB[:, 8:16], VEC[:, V_PER + V_LBL:V_PER + V_LBL + 8], VEC[:, V_LBL:V_LBL + 8], OP.subtract, ["VEC"], ["LB"])
        act(LB[:, 8:16], LB[:, 8:16], AF.Sigmoid, ["LB"], ["LB"])
        ts(OML[:, :], LB[:, :], -1.0, 1.0, OP.mult, OP.add, ["LB"], ["OML"])

        mod_done = [0, 0]

        ada_pend = []
        adr_i = [0]

        def ada_flush():
            if not ada_pend:
                return
            l, fb, adr, akey = ada_pend.pop()
            vb = l * V_PER
            pb, pbkey = next_pa()
            for f in range(2):
                mm(pb[:, 2 * f:2 * f + 2], adr[0:2, f * 128:(f + 1) * 128], CST[0:2, 0:2], True, True,
                   [akey, "CST"], [pbkey], f == 1)
            for f in range(2):
                ch = fb * 2 + f
                ts(MOD[l][:, ch, :], pb[:, 2 * f:2 * f + 2], VEC[:, vb + V_BADA + ch:vb + V_BADA + ch + 1], None,
                   OP.add, OP.bypass, [pbkey, "VEC"], [f"MOD{l}_{ch // 16}"])
            mod_done[l] = fb * 2 + 2
            if mod_done[l] == 32:
                for vi in range(2):
                    ts(A1[l][:, :, vi], MOD[l][:, 16:32, vi], 1.0, None, OP.add, OP.bypass, [f"MOD{l}_1"], [f"A1_{l}"])
                    tt(A1[l][:, :, vi], A1[l][:, :, vi], VEC[:, vb + V_NM:vb + V_NM + 16], OP.mult, [f"A1_{l}", "VEC"], [f"A1_{l}"])
            if mod_done[l] == 80:
                for vi in range(2):
                    ts(A2[l][:, :, vi], MOD[l][:, 64:80, vi], 1.0, None, OP.add, OP.bypass, [f"MOD{l}_4"], [f"A2_{l}"])
                    tt(A2[l][:, :, vi], A2[l][:, :, vi], VEC[:, vb + V_NF:vb + V_NF + 16], OP.mult, [f"A2_{l}", "VEC"], [f"A2_{l}"])

        def ada_block(l, fb):
            wv, wkey = load_w(w_ada[l], 0, 16, fb * 256, 256)
            pa, pkey = next_pa()
            for k in range(16):
                mm(pa[0:2, 0:256], SC[:, k, :], wv[:, :, k, :], k == 0, k == 15, [wkey, "SC"], [pkey], k == 15)
            i = adr_i[0] % 2
            adr_i[0] += 1
            act(ADR[i][:, :], pa[0:2, 0:256], AF.Copy, [pkey], [f"ADR{i}"])
            ada_flush()
            ada_pend.append((l, fb, ADR[i], f"ADR{i}"))

        def ada_gen():
            for l_ in range(DEPTH):
                for fb_ in range(48):
                    ada_block(l_, fb_)
                    yield
            ada_flush()
            yield

        bg = ada_gen()

        def bg_step():
            next(bg, None)

        def bg_need(l, nchunks):
            while mod_done[l] < nchunks:
                if next(bg, "END") == "END":
                    break

        out_sems = []

        seqs = [(2 * TP, TS, 1, [(0, TS, -1)]), (0, 2 * TP, 0, [(0, TP, 0), (TP, TP, 1)])]
        if 'noprompt' in flags:
            seqs = seqs[:1]
        if 'nosample' in flags:
            seqs = seqs[1:]
        if 'adaonly' in flags:
            seqs = []
        for (tok0, T, vi, segs) in seqs:
            TB = tblocks(T)
            gscope = st.enter_context(ExitStack())
            CH, HF = 32, 16
            NPB = 512 // CH
            NCH = T // CH
            xTv = xT.rearrange("(c p) t -> p c t", p=128)
            for c in range(16):
                S.dma("sp", X[:, c, 0:T], xTv[:, c, tok0:tok0 + T], w=[f"X{c}"], sem=f"lx{c % 4}")
            if vi == 1:
                pv = posT.rearrange("(c p) t -> p c t", p=128)
                for c in range(16):
                    S.dma("pool", X[:, c, 0:T], pv[:, c, 0:T], w=[f"X{c}"], sem=f"lp{c % 2}", accum=True)

            def rhsH(k, t0, tn):
                return H[:, k, t0:t0 + tn], "H"

            for l in range(DEPTH):
                vb = l * V_PER
                W_in = w_in[l]
                with ExitStack() as ms:
                    YC = sb(ms, "YC", [128, 16, T], BF16)
                    bg_need(l, 32)
                    norm_to_H(A1[l], lambda c: MOD[l][:, c, vi:vi + 1], T, vi, [f"A1_{l}", f"MOD{l}_0"])

                    big = T > 512
                    with (ExitStack() if big else nullcontext(ms)) as hs:
                        SG = sb(hs, "SG", [128, T], BF16)
                        V32 = sb(hs, "V32", [CH, NCH, 128], BF16)
                        VF = sb(hs, "VF", [128, T], BF16)
                        Q = sb(hs, "Q", [128, T], BF16)
                        O = sb(hs, "O", [128, T])
                        CE = sb(hs, "CE", [128, T + 1])
                        KK = sb(hs, "KK", [128, T], BF16)
                        DD = sb(hs, "DD", [128, T])
                        EB = sb(hs, "EB", [128, T], BF16)
                        QF = sb(hs, "QF", [128, T], BF16)
                        QR = sb(hs, "QR", [128, T], BF16)
                        KA = sb(hs, "KA", [128, T], BF16)
                        KB = sb(hs, "KB", [128, T], BF16)
                        DEC = sb(hs, "DEC", [128, NCH])
                        AL = sb(hs, "AL", [128, NCH])
                        BE = sb(hs, "BE", [128, NCH])
                        SR = sb(hs, "SR", [128, 128])
                        SAL = [sb(hs, f"SAL{i}", [128, 128], BF16) for i in range(2)]
                        KXT = [sb(hs, f"KXT{i}", [CH, 128], BF16) for i in range(2)]
                        UT = sb(hs, "UT", [128, 128])
                        ATB = [sb(hs, f"ATB{i}", [CH, CH], BF16) for i in range(4)]
                        for hh in range(0 if 'nohgrn' in flags else 4):
                            base = hh * 640

                            def v3(t_, lo=0, hi=None):
                                v = t_[:, 0:T].rearrange("p (n c) -> p n c", c=CH)
                                return v if hi is None else v[:, :, lo:hi]

                            def dir_process(d):
                                V(lambda e: e.memset(CE[:, 0:1], 0.0), [], ["CE"])
                                V(lambda e: e.tensor_tensor_scan(out=CE[:, 1:T + 1], data0=ONEF[:, 0:T],
                                                                 data1=CE[:, 1:T + 1], initial=0.0,
                                                                 op0=OP.mult, op1=OP.add), ["CE", "ONEF"], ["CE"])
                                L0 = CE[:, 0:T:CH]
                                LM = CE[:, HF:T:CH]
                                L1 = CE[:, CH:T + 1:CH]
                                tt(DEC[:, :], L1, L0, OP.subtract, ["CE"], ["DEC"])
                                act(DEC[:, :], DEC[:, :], AF.Exp, ["DEC"], ["DEC"])
                                lo_, hi_ = (AL, BE) if d == 0 else (BE, AL)
                                tt(lo_[:, :], LM, L0, OP.subtract, ["CE"], ["AB"])
                                tt(hi_[:, :], L1, LM, OP.subtract, ["CE"], ["AB"])
                                act(AL[:, :], AL[:, :], AF.Exp, ["AB"], ["AB"])
                                act(BE[:, :], BE[:, :], AF.Exp, ["AB"], ["AB"])
                                Cv = CE[:, 1:T + 1] if d == 0 else CE[:, 0:T]
                                tt(v3(DD), Cv.rearrange("p (n c) -> p n c", c=CH),
                                   LM.unsqueeze(2).to_broadcast([128, NCH, CH]), OP.subtract, ["CE"], ["DD"])
                                sQ, sK = (1.0, -1.0) if d == 0 else (-1.0, 1.0)
                                act(EB[:, 0:T], DD[:, 0:T], AF.Exp, ["DD"], ["EB"], scale=sQ)
                                tt(QF[:, 0:T], Q[:, 0:T], EB[:, 0:T], OP.mult, ["Q", "EB"], ["QF"])
                                rl, rh = (HF, CH) if d == 0 else (0, HF)
                                al, ah = (0, HF) if d == 0 else (HF, CH)
                                bl, bh = (HF, CH) if d == 0 else (0, HF)
                                S.op("pool", lambda e: e.memset(QR[:, 0:T], 0.0), [], ["QR"])
                                S.op("pool", lambda e: e.memset(KA[:, 0:T], 0.0), [], ["KA"])
                                S.op("pool", lambda e: e.memset(KB[:, 0:T], 0.0), [], ["KB"])
                                V(lambda e: e.tensor_copy(out=v3(QR, rl, rh), in_=v3(QF, rl, rh)), ["QF", "QR"], ["QR"])
                                act(EB[:, 0:T], DD[:, 0:T], AF.Exp, ["DD"], ["EB"], scale=sK)
                                tt(v3(KA, al, ah), v3(KK, al, ah), v3(EB, al, ah), OP.mult, ["KK", "EB", "KA"], ["KA"])
                                tt(v3(KB, bl, bh), v3(KK, bl, bh), v3(EB, bl, bh), OP.mult, ["KK", "EB", "KB"], ["KB"])
                                tt(v3(EB), v3(EB), BE[:, :].unsqueeze(2).to_broadcast([128, NCH, CH]), OP.mult, ["EB", "AB"], ["EB"])
                                tt(EB[:, 0:T], KK[:, 0:T], EB[:, 0:T], OP.mult, ["KK", "EB"], ["EB"])
                                MASK = MASKF if d == 0 else MASKB
                                order, seg_first, seg_last = [], {}, {}
                                for (s0_, sl_, px_) in (segs if d == 0 else segs[::-1]):
                                    cs_ = list(range(s0_ // CH, (s0_ + sl_) // CH))
                                    if d == 1:
                                        cs_ = cs_[::-1]
                                    seg_first[len(order)] = px_
                                    order.extend(cs_)
                                    seg_last[len(order) - 1] = px_
                                TBK = [(PT, "PT"), (PS_, "PS")]
                                SBK = [(PA[0], "PA0"), (PA[1], "PA1")]
                                UBK = [(PA[2], "PA2"), (PA[3], "PA3")]

                                def stage_ab(it):
                                    n = order[it]
                                    a = n * CH
                                    s2 = it % 2
                                    tb, tk = TBK[s2]
                                    mm(tb[0:CH, 0:128], EB[:, a:a + CH], IDB[:, :], True, True, ["EB", "IDB"], [tk], True)
                                    act(KXT[s2][:, :], tb[0:CH, 0:128], AF.Copy, [tk], [f"KXT{s2}"])
                                    sbk, sk = SBK[s2]
                                    mm(sbk[0:CH, 0:CH], KA[:, a:a + CH], QF[:, a:a + CH], True, False, ["KA", "QF"], [sk], False)
                                    mm(sbk[0:CH, 0:CH], KB[:, a:a + CH], QR[:, a:a + CH], False, True, ["KB", "QR"], [sk], True)
                                    tt(ATB[it % 4][:, :], sbk[0:CH, 0:CH], MASK, OP.mult, [sk, "CST"], [f"ATB{it % 4}"])
                                stage_ab(0)
                                for it, n in enumerate(order):
                                    a = n * CH
                                    s2 = it % 2
                                    s4 = it % 4
                                    if it + 1 < NCH:
                                        stage_ab(it + 1)
                                    if it in seg_first:
                                        if vi == 1:
                                            S.dma("sp", SR[:, :], st0[l, d, hh, :, :], w=["SR"], sem="ls0")
                                        else:
                                            V(lambda e: e.memset(SR[:, :], 0.0), [], ["SR"])
                                    ub, uk = UBK[s2]
                                    mm(ub[:, 0:128], KXT[s2][:, :], V32[:, n, :], True, True, [f"KXT{s2}", "V32"], [uk], True)
                                    ts(SAL[s2][:, :], SR[:, :], AL[:, n:n + 1], None, OP.mult, OP.bypass, ["SR", "AB"], [f"SAL{s2}"])
                                    stt(SR[:, :], SR[:, :], DEC[:, n:n + 1], ub[:, 0:128], OP.mult, OP.add, ["SR", "DEC", uk], ["SR"])
                                    pcol = (n % NPB) * CH
                                    mm(PO[:, pcol:pcol + CH], V32[:, n, :], ATB[s4][:, :], True, False, ["V32", f"ATB{s4}"], ["PO"], False)
                                    mm(PO[:, pcol:pcol + CH], SAL[s2][:, :], QF[:, a:a + CH], False, True, [f"SAL{s2}", "QF"], ["PO"], True)
                                    blk_done = (n % NPB == NPB - 1 or n == NCH - 1) if d == 0 else (n % NPB == 0)
                                    if blk_done:
                                        b0 = (n // NPB) * NPB * CH
                                        b1 = min(T, b0 + NPB * CH)
                                        if d == 0:
                                            act(O[:, b0:b1], PO[:, 0:b1 - b0], AF.Copy, ["PO"], ["O"])
                                        else:
                                            tt(O[:, b0:b1], PO[:, 0:b1 - b0], O[:, b0:b1], OP.add, ["PO", "O"], ["O"])
                                    if seg_last.get(it, -1) >= 0:
                                        sname = f"so{d}"
                                        S.dma("sp", nst[seg_last[it], l, d, hh, :, :], SR[:, :], r=["SR"], sem=sname)
                                        if sname not in out_sems:
                                            out_sems.append(sname)

                            def ev_qg(f, t0, tn, pa, pkey):
                                if f == 0:
                                    act(Q[:, t0:t0 + tn], pa, AF.Copy, [pkey], ["Q"])
                                else:
                                    act(SG[:, t0:t0 + tn], pa, AF.Silu, [pkey], ["SG"])

                            def ev_gate(f, t0, tn, pa, pkey):
                                col = l * 8 + f * 4 + hh
                                tm, tkey = next_tm()
                                act(tm[:, 0:tn], pa, AF.Sigmoid, [pkey], [tkey])
                                ts(tm[:, 0:tn], tm[:, 0:tn], OML[:, col:col + 1], LB[:, col:col + 1], OP.mult, OP.add,
                                   [tkey, "LB", "OML"], [tkey])
                                act(CE[:, 1 + t0:1 + t0 + tn], tm[:, 0:tn], AF.Ln, [tkey], ["CE"])
                                act(KK[:, t0:t0 + tn], tm[:, 0:tn], AF.Identity, [tkey], ["KK"],
                                    bias=ONEC[:, 0:1], scale=-1.0)
                                if t0 + tn == T:
                                    dir_process(f)
                            proj_fm(W_in, 0, 16, base + 256, 256, rhsH, T, ev_qg)
                            def ev_v(f, t0, tn, pa, pkey):
                                act(VF[:, t0:t0 + tn], pa, AF.Copy, [pkey], ["VF"])
                                if t0 + tn < T:
                                    return
                                for g4 in range(NCH // 4):
                                    bk, bkey = next_pa()
                                    for j in range(4):
                                        a = (g4 * 4 + j) * CH
                                        mm(bk[0:CH, j * 128:(j + 1) * 128], VF[:, a:a + CH], IDB[:, :], True, True,
                                           ["VF", "IDB"], [bkey], j == 3)
                                    act(V32[:, g4 * 4:g4 * 4 + 4, :], bk[0:CH, 0:512].rearrange("p (j f) -> p j f", f=128),
                                        AF.Copy, [bkey], ["V32"])
                            proj_fm(W_in, 0, 16, base + 512, 128, rhsH, T, ev_v)
                            proj_fm(W_in, 0, 16, base, 256, rhsH, T, ev_gate)
                            for (t0, tn) in TB:
                                sumsq_bcast(lambda c: (O[:, t0:t0 + tn], "O"), 1, t0, tn)
                                rstd_from(PS_[:, 0:tn], "PS", tn, 128)
                                tm, tkey = next_tm()
                                stt(tm[:, 0:tn], O[:, t0:t0 + tn], VEC[:, vb + V_HN + hh:vb + V_HN + hh + 1], RS[:, 0:tn],
                                    OP.mult, OP.mult, ["O", "VEC", "RS"], [tkey])
                                tt(YC[:, hh, t0:t0 + tn], tm[:, 0:tn], SG[:, t0:t0 + tn], OP.mult, [tkey, "SG"], ["YC"])
                        if big:
                            S.barrier()

                    with (ExitStack() if big else nullcontext(ms)) as bs:
                        NT = T // 128
                        GU = sb(bs, "GU", [128, 4, T], BF16)
                        VN = sb(bs, "VN", [128, NT, 512], BF16)
                        XS = sb(bs, "XS", [128, 512])
                        X2 = sb(bs, "X2", [128, 512])
                        SS1 = sb(bs, "SS1", [128, 1])
                        SGB = sb(bs, "SGB", [128, 512])
                        BSB = sb(bs, "BSB", [128, 512])
                        WST = sb(bs, "WST", [128, 4, 128], BF16)
                        S.dma("sp", SGB[:, :], sgn[l].partition_broadcast(128), w=["SGB"], sem="l0")
                        S.dma("sp", BSB[:, :], bsg[l].partition_broadcast(128), w=["BSB"], sem="l1")
                        S.dma("pool", WST[:, :, :], wsgT[l].rearrange("h s p -> s h p"), w=["WST"], sem="l3")

                        def gelu_to(dst, dkey, pa, pkey, n):
                            act(dst, pa, AF.Gelu_apprx_tanh, [pkey], [dkey])

                        def ev_u(off):
                            def ev(f, t0, tn, pa, pkey):
                                gelu_to(GU[:, off + f, t0:t0 + tn], "GU", pa, pkey, tn)
                            return ev
                        cb = 2560
                        proj_fm(W_in, 0, 16, cb, 256, rhsH, T, ev_u(0))
                        proj_fm(W_in, 0, 16, cb + 256, 256, rhsH, T, ev_u(2))
                        wv0, wk0 = load_w(W_in, 0, 16, cb + 512, 256)
                        wv1, wk1 = load_w(W_in, 0, 16, cb + 768, 256)
                        for n in range(NT):
                            pa, pkey = next_pa()
                            for hf, (wv, wk) in enumerate(((wv0, wk0), (wv1, wk1))):
                                for j in range(2):
                                    cj = hf * 2 + j
                                    for k in range(16):
                                        mm(pa[:, cj * 128:(cj + 1) * 128], H[:, k, n * 128:(n + 1) * 128], wv[:, j, k, :],
                                           k == 0, k == 15, [wk, "H"], [pkey], k == 15)
                            tm, tkey = next_tm()
                            gelu_to(tm[:, :], tkey, pa[:, :], pkey, 512)
                            act(X2[:, :], tm[:, :], AF.Square, [tkey], ["X2"])
                            V(lambda e: e.tensor_reduce(out=SS1[:, 0:1], in_=X2[:, :], axis=mybir.AxisListType.X, op=OP.add),
                              ["X2"], ["SS1"])
                            act(SS1[:, :], SS1[:, :], AF.Ln, ["SS1"], ["SS1"], bias=EPSC[:, 0:1], scale=1.0 / 512)
                            act(SS1[:, :], SS1[:, :], AF.Exp, ["SS1"], ["SS1"], scale=-0.5)
                            stt(VN[:, n, :], tm[:, :], SS1[:, 0:1], SGB[:, :], OP.mult, OP.mult, [tkey, "SS1", "SGB"], ["VN"])
                        for hb in range(4):
                            for n in range(NT):
                                mm(PT[:, 0:128], VN[:, n, hb * 128:(hb + 1) * 128], WST[:, hb, :], True, True,
                                   ["VN", "WST"], ["PTsgu"], True)
                                tm, tkey = next_tm()
                                tt(tm[:, 0:128], PT[:, 0:128], BSB[:, hb * 128:(hb + 1) * 128], OP.add, ["PTsgu", "BSB"], [tkey])
                                tt(YC[:, 4 + hb, n * 128:(n + 1) * 128], tm[:, 0:128], GU[:, hb, n * 128:(n + 1) * 128], OP.mult,
                                   [tkey, "GU"], ["YC"])
                        if big:
                            S.barrier()

                    with ExitStack() as cs:
                        CC = sb(cs, "CC", [128, 4, T])
                        CY = sb(cs, "CY", [128, 4, T])
                        cb = 3584

                        def ev_cc(off):
                            def ev(f, t0, tn, pa, pkey):
                                act(CC[:, off + f, t0:t0 + tn], pa, AF.Copy, [pkey], [f"CC{off + f}"])
                            return ev

                        def ev_ch(off):
                            def ev(f, t0, tn, pa, pkey):
                                j = off + f
                                tt(CC[:, j, t0:t0 + tn], pa, CC[:, j, t0:t0 + tn], OP.mult, [pkey, f"CC{j}"], [f"CC{j}"])
                                if t0 + tn == T:
                                    w0 = VEC[:, vb + V_WC + 0 * 4 + j:vb + V_WC + 0 * 4 + j + 1]
                                    w1 = VEC[:, vb + V_WC + 1 * 4 + j:vb + V_WC + 1 * 4 + j + 1]
                                    w2 = VEC[:, vb + V_WC + 2 * 4 + j:vb + V_WC + 2 * 4 + j + 1]
                                    act(CY[:, j, 0:T], CC[:, j, 0:T], AF.Copy, [f"CC{j}"], [f"CY{j}"], scale=w1)
                                    for (s0_, sl_, _px) in segs:
                                        e0_ = s0_ + sl_
                                        stt(CY[:, j, s0_ + 1:e0_], CC[:, j, s0_:e0_ - 1], w0, CY[:, j, s0_ + 1:e0_], OP.mult, OP.add, [f"CC{j}", f"CY{j}"], [f"CY{j}"])
                                        stt(CY[:, j, s0_:e0_ - 1], CC[:, j, s0_ + 1:e0_], w2, CY[:, j, s0_:e0_ - 1], OP.mult, OP.add, [f"CC{j}", f"CY{j}"], [f"CY{j}"])
                            return ev

                        def ev_cb(off):
                            def ev(f, t0, tn, pa, pkey):
                                j = off + f
                                tt(YC[:, 8 + j, t0:t0 + tn], pa, CY[:, j, t0:t0 + tn], OP.mult, [pkey, f"CY{j}"], ["YC"])
                            return ev
                        proj_fm(W_in, 0, 16, cb + 512, 256, rhsH, T, ev_cc(0))
                        proj_fm(W_in, 0, 16, cb + 768, 256, rhsH, T, ev_cc(2))
                        proj_fm(W_in, 0, 16, cb + 1024, 256, rhsH, T, ev_ch(0))
                        proj_fm(W_in, 0, 16, cb + 1280, 256, rhsH, T, ev_ch(2))
                        proj_fm(W_in, 0, 16, cb, 256, rhsH, T, ev_cb(0))
                        proj_fm(W_in, 0, 16, cb + 256, 256, rhsH, T, ev_cb(2))
                        S.barrier()

                    with ExitStack() as dsx:
                        DX = sb(dsx, "DX", [128, T])
                        CS = sb(dsx, "CS", [128, T + 1])
                        SM = sb(dsx, "SM", [128, T])
                        DF = sb(dsx, "DF", [128, T], BF16)
                        WP = sb(dsx, "WP", [128, 4, 128], BF16)
                        S.dma("pool", WP[:, :, :], wpool[l].rearrange("g i o -> i g o"), w=["WP"], sem="l3")
                        cb = 5120

                        def ev_d(off):
                            def ev(f, t0, tn, pa, pkey):
                                gi = off + f
                                win = POOLW[gi]
                                hw = win // 2
                                act(DX[:, t0:t0 + tn], pa, AF.Copy, [pkey], ["DX"])
                                if t0 + tn < T:
                                    return
                                for (s0_, L_, _px) in segs:
                                    V(lambda e: e.memset(CS[:, 0:1], 0.0), [], ["CS"])
                                    V(lambda e: e.tensor_tensor_scan(out=CS[:, 1:L_ + 1], data0=ONEF[:, 0:L_], data1=DX[:, s0_:s0_ + L_],
                                                                     initial=0.0, op0=OP.mult, op1=OP.add), ["DX", "ONEF"], ["CS"])
                                    SMs = SM[:, s0_:s0_ + L_]
                                    tt(SMs[:, hw:L_ - hw + 1], CS[:, 2 * hw:L_ + 1], CS[:, 0:L_ - 2 * hw + 1], OP.subtract, ["CS"], ["SM"])
                                    V(lambda e: e.tensor_copy(out=SMs[:, 0:hw], in_=CS[:, hw:2 * hw]), ["CS"], ["SM"])
                                    if hw > 1:
                                        ts(SMs[:, L_ - hw + 1:L_], CS[:, L_ - 2 * hw + 1:L_ - hw], -1.0, CS[:, L_:L_ + 1], OP.mult, OP.add, ["CS"], ["SM"])
                                    ts(SMs[:, hw:L_ - hw + 1], SMs[:, hw:L_ - hw + 1], 1.0 / win, None, OP.mult, OP.bypass, ["SM"], ["SM"])
                                    for t in range(hw):
                                        ts(SMs[:, t:t + 1], SMs[:, t:t + 1], 1.0 / (t + hw), None, OP.mult, OP.bypass, ["SM"], ["SM"])
                                    for t in range(L_ - hw + 1, L_):
                                        ts(SMs[:, t:t + 1], SMs[:, t:t + 1], 1.0 / (L_ - t + hw), None, OP.mult, OP.bypass, ["SM"], ["SM"])
                                tt(DF[:, 0:T], SM[:, 0:T], DX[:, 0:T], OP.subtract, ["SM", "DX"], ["DF"])
                                for (u0, un) in TB:
                                    pa2, pk2 = next_pa()
                                    mm(pa2[:, 0:un], WP[:, gi, :], DF[:, u0:u0 + un], True, True, ["WP", "DF"], [pk2], True)
                                    act(YC[:, 12 + gi, u0:u0 + un], pa2[:, 0:un], AF.Copy, [pk2, "VEC"], ["YC"],
                                        scale=VEC[:, vb + V_PS + gi:vb + V_PS + gi + 1])
                            return ev
                        proj_fm(W_in, 0, 16, cb, 256, rhsH, T, ev_d(0))
                        proj_fm(W_in, 0, 16, cb + 256, 256, rhsH, T, ev_d(2))
                        if big:
                            S.barrier()

                    def rhsY(k, t0, tn):
                        return YC[:, k, t0:t0 + tn], "YC"

                    def ev_out(db):
                        def ev(f, t0, tn, pa, pkey):
                            dch = db * 2 + f
                            stt(X[:, dch, t0:t0 + tn], pa, MOD[l][:, 32 + dch, vi:vi + 1], X[:, dch, t0:t0 + tn],
                                OP.mult, OP.add, [pkey, f"MOD{l}_2", f"X{dch}"], [f"X{dch}"])
                        return ev
                    bg_need(l, 48)
                    for db in range(8):
                        proj_fm(w_out[l], 0, 16, db * 256, 256, rhsY, T, ev_out(db))
                    S.barrier()

                bg_need(l, 80)
                norm_to_H(A2[l], lambda c: MOD[l][:, 48 + c, vi:vi + 1], T, vi, [f"A2_{l}", f"MOD{l}_3"])
                qoff = 0
                with ExitStack() as fs:
                  G = sb(fs, "G", [128, 12, T], BF16)
                  CVV = sb(fs, "CVV", [128, 2, T])
                  CVG = [sb(fs, f"CVG{i}", [128, T]) for i in range(2)]
                  cvg_i = [0]
                  WB.append(sb(fs, "WBx", [128, 4096], BF16))
                  WB.append(sb(fs, "WBy", [128, 4096], BF16))
                  for nq in (() if 'noffn' in flags else (12, 12, 10, 10)):
                    if True:

                        def ev_up(is_gate, i0):
                            pend = []

                            def ev(f, t0, tn, pa, pkey):
                                i = i0 + f
                                ch = (44 if is_gate else 0) + qoff + i
                                w0 = VEC[:, vb + V_WCF + 0 * 88 + ch:vb + V_WCF + 0 * 88 + ch + 1]
                                w1 = VEC[:, vb + V_WCF + 1 * 88 + ch:vb + V_WCF + 1 * 88 + ch + 1]
                                w2 = VEC[:, vb + V_WCF + 2 * 88 + ch:vb + V_WCF + 2 * 88 + ch + 1]
                                if is_gate:
                                    if t0 == 0:
                                        cvg_i[0] += 1
                                    gi_ = cvg_i[0] % 2
                                    cvt, ck = CVG[gi_][:, :], f"CVG{gi_}"
                                else:
                                    cvt, ck = CVV[:, f, :], f"CVV{f}"
                                act(cvt[:, t0:t0 + tn], pa, AF.Copy, [pkey, "VEC"], [ck], scale=w1)
                                pend.append((t0, tn, pa, pkey))
                                if t0 + tn < T:
                                    return
                                for (s0_, sl_, _px) in segs:
                                    e0_ = s0_ + sl_
                                    for bi, (b0, bn, pb, pbk) in enumerate(pend):
                                        lo, hi = max(s0_, b0), min(e0_, b0 + bn)
                                        if hi - lo < 2:
                                            continue
                                        stt(cvt[:, lo + 1:hi], pb[:, lo - b0:hi - 1 - b0], w0, cvt[:, lo + 1:hi], OP.mult, OP.add, [pbk, ck], [ck])
                                        stt(cvt[:, lo:hi - 1], pb[:, lo + 1 - b0:hi - b0], w2, cvt[:, lo:hi - 1], OP.mult, OP.add, [pbk, ck], [ck])
                                        if bi + 1 < len(pend) and hi < e0_ and hi == b0 + bn:
                                            nb0, nbn, npb, npbk = pend[bi + 1]
                                            stt(cvt[:, hi:hi + 1], pb[:, bn - 1:bn], w0, cvt[:, hi:hi + 1], OP.mult, OP.add, [pbk, ck], [ck])
                                            stt(cvt[:, hi - 1:hi], npb[:, 0:1], w2, cvt[:, hi - 1:hi], OP.mult, OP.add, [npbk, ck], [ck])
                                pend.clear()
                                if is_gate:
                                    act(cvt[:, 0:T], cvt[:, 0:T], AF.Silu, [ck], [ck])
                                    tt(G[:, i, 0:T], CVV[:, f, 0:T], cvt[:, 0:T], OP.mult, [f"CVV{f}", ck], ["G"])
                            return ev
                        for i0 in range(0, nq, 2):
                            proj_fm(w_up[l], 0, 16, (qoff + i0) * 128, 256, rhsH, T, ev_up(False, i0))
                            proj_fm(w_up[l], 0, 16, DFF + (qoff + i0) * 128, 256, rhsH, T, ev_up(True, i0))

                        def rhsG(k, t0, tn):
                            return G[:, k, t0:t0 + tn], "G"

                        def ev_dn(db):
                            def ev(f, t0, tn, pa, pkey):
                                dch = db * 2 + f
                                stt(X[:, dch, t0:t0 + tn], pa, MOD[l][:, 80 + dch, vi:vi + 1], X[:, dch, t0:t0 + tn],
                                    OP.mult, OP.add, [pkey, f"MOD{l}_5", f"X{dch}"], [f"X{dch}"])
                            return ev
                        bg_need(l, 96)
                        for db in range(8):
                            proj_fm(w_down[l], qoff, nq, db * 256, 256, rhsG, T, ev_dn(db))
                    qoff += nq
                  S.barrier()
                  WB.pop()
                  WB.pop()

            yTv = yT.rearrange("(c p) t -> p c t", p=128)
            for (t0, tn) in TB:
                sumsq_bcast(lambda c: (X[:, c, t0:t0 + tn], f"X{c}"), 16, t0, tn)
                rstd_from(PS_[:, 0:tn], "PS", tn, D)
                for c in range(16):
                    tm, tkey = next_tm()
                    stt(tm[:, 0:tn], X[:, c, t0:t0 + tn], VEC[:, V_FIN + c:V_FIN + c + 1], RS[:, 0:tn], OP.mult, OP.mult,
                        [f"X{c}", "VEC", "RS"], [tkey])
                    sname = f"sy{c % 2}"
                    S.dma("sp", yTv[:, c, tok0 + t0:tok0 + t0 + tn], tm[:, 0:tn], r=[tkey], sem=sname)
                    if sname not in out_sems:
                        out_sems.append(sname)
            S.barrier()
            gscope.close()

        need = {s_: S.dsem[s_][1] for s_ in out_sems}
        S._wait("sp", need)
    return nc


def _fm(v):
    v = np.asarray(v, np.float32)
    return np.ascontiguousarray(v.reshape(-1, 128).T)


def _pos_embed_T():
    rows = TS // 64
    r = np.repeat(np.arange(rows), 64).astype(np.float32)[:, None]
    col = np.tile(np.arange(64), rows).astype(np.float32)[:, None]
    quarter = D // 4
    freq = np.exp(np.float32(-np.log(10000.0)) * np.arange(quarter, dtype=np.float32) / np.float32(quarter)).astype(np.float32)[None, :]
    emb = np.concatenate([np.sin(r * freq), np.cos(r * freq), np.sin(col * freq), np.cos(col * freq)], -1).astype(np.float32)
    return np.ascontiguousarray(emb.T)


_PROG = {}


def kernel(x_prompt, x_sample, c, state_hgrn, c_ctx, w_ada, b_ada, norm_mix, norm_ffn, w_in,
           lb_logits, hgrn_norm, sgu_norm, w_sgu, b_sgu, w_conv_c, w_pool, pool_scale, w_out,
           w_up, w_conv_ffn, w_down, norm_final):
    f = lambda a: np.asarray(a, np.float32)
    x_prompt, x_sample, c, state_hgrn, c_ctx = f(x_prompt), f(x_sample), f(c), f(state_hgrn), f(c_ctx)
    w_ada, w_in, w_out, w_up, w_down = f(w_ada), f(w_in), f(w_out), f(w_up), f(w_down)
    vecs = np.zeros((128, NV), np.float32)
    for l in range(DEPTH):
        vb = l * V_PER
        vecs[:, vb + V_BADA:vb + V_BADA + 96] = _fm(b_ada[l])
        vecs[:, vb + V_NM:vb + V_NM + 16] = _fm(norm_mix[l])
        vecs[:, vb + V_NF:vb + V_NF + 16] = _fm(norm_ffn[l])
        vecs[:, vb + V_HN:vb + V_HN + 4] = _fm(hgrn_norm[l])
        wc = f(w_conv_c)[l]
        for tap in range(3):
            vecs[:, vb + V_WC + tap * 4:vb + V_WC + tap * 4 + 4] = _fm(wc[tap])
        vecs[:, vb + V_PS:vb + V_PS + 4] = _fm(pool_scale[l])
        wcf = f(w_conv_ffn)[l]
        for tap in range(3):
            vecs[:, vb + V_WCF + tap * 88:vb + V_WCF + tap * 88 + 88] = _fm(wcf[tap])
        lbl = f(lb_logits)[l]
        for d in range(2):
            vecs[:, vb + V_LBL + d * 4:vb + V_LBL + d * 4 + 4] = _fm(lbl[d])
    vecs[:, V_FIN:V_FIN + 16] = _fm(norm_final)
    consts = np.zeros((128, 384), np.float32)
    consts[:, 0:128] = np.eye(128, dtype=np.float32)
    ii = np.arange(64)
    consts[0:64, 128:192] = (ii[None, :] >= ii[:, None]).astype(np.float32)
    consts[0:64, 256:320] = (ii[:, None] >= ii[None, :]).astype(np.float32)
    perm = []
    for h in range(4):
        for grp in (2, 3, 0, 4, 1):
            perm.extend(range(grp * 512 + h * 128, grp * 512 + (h + 1) * 128))
    perm.extend(range(2560, 5632))
    def tile_w(w):
        L, K, N = w.shape
        return np.ascontiguousarray(w.reshape(L, K // 128, 128, N // 128, 128).transpose(0, 3, 2, 1, 4)).reshape(L, N // 128, 128, K)
    w_in_p = tile_w(w_in[:, :, np.asarray(perm)])
    w_ada, w_out, w_up, w_down = tile_w(w_ada), tile_w(w_out), tile_w(w_up), tile_w(w_down)
    wsgT = np.ascontiguousarray(np.transpose(f(w_sgu), (0, 1, 3, 2)))
    posT = _pos_embed_T()
    sgn = np.ascontiguousarray(f(sgu_norm))
    bsg = np.ascontiguousarray(f(b_sgu).reshape(DEPTH, 512))
    wpl = np.ascontiguousarray(f(w_pool))

    in_maps = []
    for core in range(NCORES):
        b = core % 2
        xT = np.ascontiguousarray(np.concatenate(
            [x_prompt[2 * core].T, x_prompt[2 * core + 1].T, x_sample[b].T], axis=1))
        cvv = np.stack([_fm(c_ctx), _fm(c[b])], axis=-1)
        in_maps.append(dict(xT=xT, posT=posT, cv=np.ascontiguousarray(cvv), st0=np.ascontiguousarray(state_hgrn[b]),
                            vecs=vecs, sgn=sgn, bsg=bsg, wsgT=wsgT, wpool=wpl, consts=consts,
                            w_ada=w_ada, w_in=w_in_p, w_out=w_out, w_up=w_up, w_down=w_down))
    if "nc" not in _PROG:
        _PROG["nc"] = build_program()
    res = run_bass_kernel_spmd(_PROG["nc"], in_maps, core_ids=list(range(NCORES)))
    y_prompt = np.zeros((16, TP, D), np.float32)
    y_sample = np.zeros((2, TS, D), np.float32)
    new_state = np.zeros((16, DEPTH, 2, 4, 128, 128), np.float32)
    for core in range(NCORES):
        r = res.results[core]
        yT = r["yT"]
        y_prompt[2 * core] = yT[:, 0:TP].T
        y_prompt[2 * core + 1] = yT[:, TP:2 * TP].T
        if core < 2:
            y_sample[core] = yT[:, 2 * TP:].T
        new_state[2 * core] = r["nst"][0]
        new_state[2 * core + 1] = r["nst"][1]
    return (y_prompt, y_sample, new_state)
```
